# Optimizing a Trainium2 kernel written in Bass

```python
import jax, jax.numpy as jnp
from jax import lax
import numpy as np

D_MODEL = 2048
BATCH = 2
SEQ = 4096
DEPTH = 1

CHUNK = 64
D_FF = 5632
W_CONV = D_MODEL // 2
CONV_K = 31
W_POOL = D_MODEL // 2
POOL_WINDOWS = (2, 4, 8, 16)
N_POOL_GROUPS = len(POOL_WINDOWS)
POOL_GROUP_IN = W_POOL // N_POOL_GROUPS
POOL_GROUP_OUT = D_MODEL // N_POOL_GROUPS
N_BRANCHES = 2
N_IN_COLS = 2 * W_CONV + W_POOL + N_BRANCHES * D_MODEL
N_ADA = 3
EPS = 1e-6

kernel_name = "hybrid_conformer_pool_gated_block"


def rms_norm(x, g):
    xf = x.astype(jnp.float32)
    y = xf * lax.rsqrt(jnp.mean(xf * xf, axis=-1, keepdims=True) + EPS)
    return y.astype(x.dtype) * g


def layer_norm(x, g, b):
    xf = x.astype(jnp.float32)
    mu = jnp.mean(xf, axis=-1, keepdims=True)
    var = jnp.mean(jnp.square(xf - mu), axis=-1, keepdims=True)
    y = (xf - mu) * lax.rsqrt(var + EPS)
    return y.astype(x.dtype) * g + b


def modulate(n, shift, scale):
    return n * (1.0 + scale[:, None, :]) + shift[:, None, :]


def swiglu(n, w_in, w_out):
    hu = n @ w_in
    h, u = jnp.split(hu, 2, axis=-1)
    return (jax.nn.silu(h) * u) @ w_out


def causal_mean_pool(v, window):
    seq = v.shape[1]
    vf = v.astype(jnp.float32)
    csum = jnp.cumsum(vf, axis=1)
    lagged = jnp.pad(csum, ((0, 0), (window, 0), (0, 0)))[:, :seq]
    count = jnp.minimum(jnp.arange(1, seq + 1, dtype=jnp.float32), float(window))
    return ((csum - lagged) / count[None, :, None]).astype(v.dtype)


def conformer_conv_branch(glu_in, conv_w, conv_b, ln_a_g, ln_a_b, w_a_out, b_a_out):
    a, g = jnp.split(glu_in, 2, axis=-1)
    a = a * jax.nn.sigmoid(g)
    a = lax.conv_general_dilated(
        a, conv_w[:, None, :], window_strides=(1,), padding=[(CONV_K - 1, 0)],
        dimension_numbers=('NWC', 'WIO', 'NWC'), feature_group_count=W_CONV) + conv_b
    a = jax.nn.silu(layer_norm(a, ln_a_g, ln_a_b))
    return a @ w_a_out + b_a_out


def pool_branch(v, w_b_group, b_b_group, ls_b):
    bsz, seq, _ = v.shape
    groups = jnp.split(v, N_POOL_GROUPS, axis=-1)
    mixed = jnp.stack([causal_mean_pool(vg, w) - vg for vg, w in zip(groups, POOL_WINDOWS)], axis=2)
    y = jnp.einsum('bsgc,gco->bsgo', mixed, w_b_group) + b_b_group
    return y.reshape(bsz, seq, D_MODEL) * ls_b


def setup_inputs(seed: int = 0) -> dict:
    key = jax.random.key(seed)
    ks = jax.random.split(key, 32)
    f = jnp.float32
    D = D_MODEL

    def nrm(k, shape, scale):
        return jax.random.normal(k, shape, f) * scale

    def gain(k, shape):
        return 1.0 + 0.05 * jax.random.normal(k, shape, f)

    return {
        'x': nrm(ks[0], (BATCH, SEQ, D), 1.0),
        'c': nrm(ks[1], (BATCH, D), 1.0),
        'w_ada': nrm(ks[2], (D, N_ADA * 3 * D), 0.5 * D ** -0.5),
        'b_ada': nrm(ks[3], (N_ADA * 3 * D,), 0.01),
        'g_ffn1': gain(ks[4], (D,)),
        'w1_in': nrm(ks[5], (D, 2 * D_FF), D ** -0.5),
        'w1_out': nrm(ks[6], (D_FF, D), D_FF ** -0.5),
        'g_mix': gain(ks[7], (D,)),
        'w_in': nrm(ks[8], (D, N_IN_COLS), D ** -0.5),
        'conv_w': nrm(ks[9], (CONV_K, W_CONV), CONV_K ** -0.5),
        'conv_b': nrm(ks[10], (W_CONV,), 0.01),
        'ln_a_g': gain(ks[11], (W_CONV,)),
        'ln_a_b': nrm(ks[12], (W_CONV,), 0.01),
        'w_a_out': nrm(ks[13], (W_CONV, D), W_CONV ** -0.5),
        'b_a_out': nrm(ks[14], (D,), 0.01),
        'w_b_group': nrm(ks[15], (N_POOL_GROUPS, POOL_GROUP_IN, POOL_GROUP_OUT), POOL_GROUP_IN ** -0.5),
        'b_b_group': nrm(ks[16], (N_POOL_GROUPS, POOL_GROUP_OUT), 0.01),
        'ls_b': gain(ks[17], (D,)),
        'w_out': nrm(ks[18], (D, D), D ** -0.5),
        'g_ffn2': gain(ks[19], (D,)),
        'w2_in': nrm(ks[20], (D, 2 * D_FF), D ** -0.5),
        'w2_out': nrm(ks[21], (D_FF, D), D_FF ** -0.5),
        'g_final': gain(ks[22], (D,)),
    }


def reference(x, c, w_ada, b_ada, g_ffn1, w1_in, w1_out, g_mix, w_in, conv_w, conv_b,
              ln_a_g, ln_a_b, w_a_out, b_a_out, w_b_group, b_b_group, ls_b, w_out,
              g_ffn2, w2_in, w2_out, g_final):
    bsz = x.shape[0]
    ada = (jax.nn.silu(c) @ w_ada + b_ada).reshape(bsz, N_ADA, 3, D_MODEL)
    h = x
    for _ in range(DEPTH):
        n = modulate(rms_norm(h, g_ffn1), ada[:, 0, 0], ada[:, 0, 1])
        h = h + 0.5 * ada[:, 0, 2][:, None, :] * swiglu(n, w1_in, w1_out)

        n = modulate(rms_norm(h, g_mix), ada[:, 1, 0], ada[:, 1, 1])
        proj = n @ w_in
        glu_in = proj[..., :2 * W_CONV]
        pool_in = proj[..., 2 * W_CONV:2 * W_CONV + W_POOL]
        gate_a = jax.nn.sigmoid(proj[..., 2 * W_CONV + W_POOL:2 * W_CONV + W_POOL + D_MODEL])
        gate_b = jax.nn.sigmoid(proj[..., 2 * W_CONV + W_POOL + D_MODEL:])
        y_a = conformer_conv_branch(glu_in, conv_w, conv_b, ln_a_g, ln_a_b, w_a_out, b_a_out)
        y_b = pool_branch(pool_in, w_b_group, b_b_group, ls_b)
        mix = (gate_a * y_a + gate_b * y_b) @ w_out
        h = h + ada[:, 1, 2][:, None, :] * mix

        n = modulate(rms_norm(h, g_ffn2), ada[:, 2, 0], ada[:, 2, 1])
        h = h + 0.5 * ada[:, 2, 2][:, None, :] * swiglu(n, w2_in, w2_out)
    return rms_norm(h, g_final)
```

```python
import numpy as np
import concourse.bass as bass
import concourse.mybir as mybir
from concourse.bass_utils import run_bass_kernel_spmd

F32 = mybir.dt.float32
BF16 = mybir.dt.bfloat16
AF = mybir.ActivationFunctionType
ALU = mybir.AluOpType

P = 128
D = 2048
KC = 16
HALO = 32
TOWN = 1024
T = TOWN + HALO
DFF = 5632
NH = DFF // P
WC = 1024
NCORES = 8
EPS = 1e-6
POOL_W = (2, 4, 8, 16)

C_C = 0
C_BADA = 16
C_G1 = 160
C_GM = 176
C_G2 = 192
C_GF = 208
C_CB = 224
C_LG = 232
C_LB = 240
C_BA = 248
C_BB = 264
C_LS = 280
C_CW = 296
C_HM = 544
C_CNT = 545
C_ID = 609
NPP = 737

NSLOT = 6
SLOT_E = 4096
UW = 10560


class Sched:
    ENGS = ("pe", "act", "dve", "pool", "sp")

    def __init__(self):
        self.ops = {e: [] for e in self.ENGS}
        self.acc = {}
        self.waited = {e: {} for e in self.ENGS}
        self.dmacnt = {}

    def add(self, eng, emit, reads=(), writes=(), dma=None):
        idx = len(self.ops[eng])
        if dma is not None:
            self.dmacnt[dma] = self.dmacnt.get(dma, 0) + 16
            token = ("d", dma, self.dmacnt[dma])
        else:
            token = ("e", eng, idx)
        deps = set()
        for (name, lo, hi) in reads:
            for rec in self.acc.get(name, ()):
                if rec[2] and rec[0] < hi and lo < rec[1]:
                    deps.add(rec[3])
        for (name, lo, hi) in writes:
            for rec in self.acc.get(name, ()):
                if rec[0] < hi and lo < rec[1]:
                    deps.add(rec[3])
        for (name, lo, hi) in writes:
            lst = self.acc.setdefault(name, [])
            lst[:] = [r for r in lst if not (lo <= r[0] and r[1] <= hi)]
            lst.append((lo, hi, True, token))
        for (name, lo, hi) in reads:
            lst = self.acc.setdefault(name, [])
            if token[0] == "e":
                lst[:] = [r for r in lst if not ((not r[2]) and r[3][0] == "e" and r[3][1] == eng
                                                 and lo <= r[0] and r[1] <= hi)]
            lst.append((lo, hi, False, token))
        best = {}
        for tok in deps:
            if tok[0] == "e":
                if tok[1] == "pe" and eng == "pe":
                    continue
                key = ("e", tok[1])
            else:
                if dma is not None and tok[1] == dma:
                    continue
                key = ("d", tok[1])
            if best.get(key, -1) < tok[2]:
                best[key] = tok[2]
        waits = []
        for key, val in best.items():
            if self.waited[eng].get(key, -1) >= val:
                continue
            self.waited[eng][key] = val
            waits.append((key, val))
        self.ops[eng].append({"emit": emit, "waits": waits, "ms": False, "dma": dma})
        return token

    def finalize(self):
        for e in self.ENGS:
            for op in self.ops[e]:
                for (key, val) in op["waits"]:
                    if key[0] == "e":
                        self.ops[key[1]][val]["ms"] = True
        self.msnum = {}
        for e in self.ENGS:
            c = 0
            arr = []
            for op in self.ops[e]:
                if op["ms"]:
                    c += 1
                arr.append(c)
            self.msnum[e] = arr


class Buf:
    def __init__(self, ap3, space, base_bytes, width, esz):
        self.ap3 = ap3
        self.space = space
        self.base = base_bytes
        self.width = width
        self.esz = esz

    def r(self, c, a=0, b=None):
        if b is None:
            b = self.width
        lo = self.base + (c * self.width + a) * self.esz
        hi = self.base + (c * self.width + b) * self.esz
        return (self.space, lo, hi)

    def rr(self, c0, c1):
        return (self.space, self.base + c0 * self.width * self.esz, self.base + c1 * self.width * self.esz)


def build_nc():
    nc = bass.Bass("TRN2", target_bir_lowering=False)
    xT = nc.dram_tensor("xT", [D, T], F32, kind="ExternalInput").ap()
    ppd = nc.dram_tensor("pp", [P, NPP], F32, kind="ExternalInput").ap()
    w_ada = nc.dram_tensor("w_ada", [16, P, 8, 512], F32, kind="ExternalInput").ap()
    w_adaD = nc.dram_tensor("w_adaD", [56, P, 2, D], F32, kind="ExternalInput").ap()
    cbc = nc.dram_tensor("cbc", [P, D], F32, kind="ExternalInput").ap()
    w1_in = nc.dram_tensor("w1_in", [44, P, KC, 256], F32, kind="ExternalInput").ap()
    w1_out = nc.dram_tensor("w1_out", [8, P, NH, 256], F32, kind="ExternalInput").ap()
    w_in = nc.dram_tensor("w_in", [28, P, KC, 256], F32, kind="ExternalInput").ap()
    w_a_out = nc.dram_tensor("w_a_out", [8, P, 8, 256], F32, kind="ExternalInput").ap()
    w_b = nc.dram_tensor("w_b", [8, P, 2, 256], F32, kind="ExternalInput").ap()
    w_out = nc.dram_tensor("w_out", [8, P, KC, 256], F32, kind="ExternalInput").ap()
    w2_in = nc.dram_tensor("w2_in", [44, P, KC, 256], F32, kind="ExternalInput").ap()
    w2_out = nc.dram_tensor("w2_out", [8, P, NH, 256], F32, kind="ExternalInput").ap()
    outT = nc.dram_tensor("outT", [D, TOWN], F32, kind="ExternalOutput").ap()

    from contextlib import ExitStack
    es = ExitStack()
    with es:
        def sb(name, shape, dt):
            return es.enter_context(nc.sbuf_tensor(name, shape, dt))

        hT = sb("hT", [P, KC, T], F32)
        nT = sb("nT", [P, KC, T], BF16)
        ring = sb("ring", [P, NSLOT, SLOT_E], BF16)
        U = sb("U", [P, UW], F32)
        pp = sb("pp_sb", [P, NPP], F32)
        adaT = sb("adaT", [P, 144], F32)
        gsT = sb("gsT", [P, 48], F32)
        ghT = sb("ghT", [P, 48], F32)
        identb = sb("identb", [P, P], BF16)
        ones_rms = sb("ones_rms", [P, P], BF16)
        ones_ln = sb("ones_ln", [P, P], BF16)
        ones32 = sb("ones32", [P, P], F32)
        scb = sb("scb", [P, KC], BF16)
        rs = sb("rs", [P, T], F32)
        tmp = sb("tmp", [P, 4, 512], F32)
        psum = [es.enter_context(nc.psum_tensor("ps%d" % i, [P, 512], F32)) for i in range(8)]

        sem_e = {e: es.enter_context(nc.semaphore("sem_" + e)) for e in ("pe", "act", "dve", "pool")}
        sem_slot = [es.enter_context(nc.semaphore("sem_slot%d" % i)) for i in range(NSLOT)]
        sem_x = [es.enter_context(nc.semaphore("sem_x%d" % i)) for i in range(4)]
        sem_pp = es.enter_context(nc.semaphore("sem_pp"))
        sem_cb = es.enter_context(nc.semaphore("sem_cb"))
        sem_o = es.enter_context(nc.semaphore("sem_o"))
        dma_sems = {"pp": sem_pp, "o": sem_o, "cb": sem_cb}
        for i in range(4):
            dma_sems[("x", i)] = sem_x[i]
        for i in range(NSLOT):
            dma_sems[("slot", i)] = sem_slot[i]

        S = Sched()

        Bh = Buf(hT, "h", 0, T, 4)
        Bn = Buf(nT, "n", 0, T, 2)
        Brs = Buf(None, "rs", 0, T, 4)
        Bpp = Buf(None, "pp", 0, NPP, 4)
        Bada = Buf(None, "ada", 0, 144, 4)
        Bgs = Buf(None, "gs", 0, 48, 4)
        Bgh = Buf(None, "gh", 0, 48, 4)
        Btmp = Buf(tmp, "tmp", 0, 512, 4)

        def uview(w0, nchunk, width, dt):
            esz = 4 if dt == F32 else 2
            nwords = nchunk * width * esz // 4
            assert w0 + nwords <= UW, (w0, nwords)
            ap = U[:, w0:w0 + nwords]
            if dt != F32:
                ap = ap.bitcast(dt)
            ap = ap.rearrange("p (c t) -> p c t", c=nchunk)
            return Buf(ap, "U", w0 * 4, width, esz)

        Bsq = uview(0, KC, T, BF16)
        Bp = Bsq
        Btn = uview(8448, 2, T, F32)
        MW = 544
        Babf = uview(0, 8, MW, BF16)
        Bz = uview(0, 8, 512, BF16)
        Bcv = uview(2176, 8, 512, F32)
        Bdg = uview(6272, 31, 128, BF16)
        _dg1 = tmp[:].rearrange("p a b -> p (a b)")[:, 0:1984].bitcast(BF16).rearrange("p (c t) -> p c t", c=31)
        Bdg1 = Buf(_dg1, "tmp", 0, 128, 2)
        Bdgs = (Bdg, Bdg1)
        Bsqm = uview(8320, 8, 512, BF16)
        Bm = uview(6272, 16, 512, BF16)
        Bln = uview(6272, 3, 512, F32)
        Bv = uview(2176, 3, MW, F32)
        Btf = uview(3808, 1, 16, F32)
        Bmx = uview(4096, 8, 512, BF16)

        bank_ctr = [0]

        ADA_BANK = 7
        nbanks = [7]

        def new_bank():
            b = bank_ctr[0] % nbanks[0]
            bank_ctr[0] += 1
            return b

        def psr(b, a=0, w=512):
            return (("ps", b), 0, 2048)

        def op_mm(out_ap, lhsT, rhs, start, stop, reads, writes, skip=False):
            S.add("pe", lambda e, o=out_ap, l=lhsT, r=rhs, st=start, sp=stop, sk=skip:
                  e.matmul(o, l, r, start=st, stop=sp, skip_group_check=sk),
                  reads=reads, writes=writes)

        def op_act(out_ap, in_ap, func, reads, writes, bias=None, scale=None):
            def emit(e, o=out_ap, i=in_ap, f=func, b=bias, s=scale):
                kw = {}
                if b is not None:
                    kw["bias"] = b
                if s is not None:
                    kw["scale"] = s
                return e.activation(o, i, f, **kw)
            S.add("act", emit, reads=reads, writes=writes)

        def op_tt(out_ap, in0, in1, op, reads, writes):
            S.add("dve", lambda e, o=out_ap, a=in0, b=in1, p=op: e.tensor_tensor(o, a, b, p), reads=reads, writes=writes)

        def op_ts(out_ap, in0, s1, s2, op0, op1, reads, writes):
            def emit(e, o=out_ap, a=in0, x=s1, y=s2, p0=op0, p1=op1):
                if p1 is None:
                    return e.tensor_scalar(o, a, x, None, p0)
                return e.tensor_scalar(o, a, x, y, p0, p1)
            S.add("dve", emit, reads=reads, writes=writes)

        def op_stt(out_ap, in0, scalar, in1, op0, op1, reads, writes):
            S.add("dve", lambda e, o=out_ap, a=in0, s=scalar, b=in1, p0=op0, p1=op1:
                  e.scalar_tensor_tensor(o, a, s, b, p0, p1), reads=reads, writes=writes)

        def op_copy(out_ap, in_ap, reads, writes):
            S.add("dve", lambda e, o=out_ap, i=in_ap: e.tensor_copy(o, i), reads=reads, writes=writes)

        def op_recip(out_ap, in_ap, reads, writes):
            S.add("dve", lambda e, o=out_ap, i=in_ap: e.reciprocal(o, i), reads=reads, writes=writes)

        def op_memset(ap, val, writes):
            S.add("dve", lambda e, a=ap, v=val: e.memset(a, v), writes=writes)

        slot_ctr = [0]

        def fill_slot(pieces):
            s = slot_ctr[0] % NSLOT
            slot_ctr[0] += 1
            for (src, kn, ncols, eoff) in pieces:
                dst = ring[:, s, eoff:eoff + kn * ncols].rearrange("p (k c) -> p k c", k=kn)
                srcv = src
                S.add("pool", lambda e, o=dst, i=srcv: e.dma_start(out=o, in_=i, max_dma_last_dim=8192),
                      writes=[(("ring", s), eoff * 2, (eoff + kn * ncols) * 2)], dma=("slot", s))
            return s

        def slot_view(s, kn, ncols, eoff=0):
            return ring[:, s, eoff:eoff + kn * ncols].rearrange("p (k c) -> p k c", k=kn)

        def slot_r(s):
            return (("ring", s), 0, SLOT_E * 2)

        S.add("sp", lambda e: e.dma_start(out=pp[:], in_=ppd), writes=[Bpp.r(0)], dma="pp")
        for g4 in range(4):
            S.add("sp", lambda e, g4=g4: e.dma_start(
                out=hT[:, g4 * 4:(g4 + 1) * 4, :],
                in_=xT[g4 * 512:(g4 + 1) * 512, :].rearrange("(k p) t -> p k t", p=P)),
                writes=[Bh.rr(g4 * 4, g4 * 4 + 4)], dma=("x", g4))
        S.add("sp", lambda e: e.dma_start(out=tmp[:].rearrange("p a b -> p (a b)"), in_=cbc), writes=[Btmp.rr(0, 4)], dma="cb")
        op_memset(ones_rms[:], 1.0 / D, [("ones_rms", 0, 1)])
        op_memset(ones_ln[:], 1.0 / WC, [("ones_ln", 0, 1)])
        op_memset(ones32[:], 1.0 / WC, [("ones32", 0, 1)])
        op_copy(identb[:], pp[:, C_ID:C_ID + P], [Bpp.r(0, C_ID, C_ID + P)], [("identb", 0, 1)])
        op_act(scb[:], pp[:, C_C:C_C + KC], AF.Silu, [Bpp.r(0, C_C, C_C + KC)], [("scb", 0, 1)])

        ada_first = [True]
        tickq = []

        def ada_item(s, cb, kh):
            c0 = s * 48 * P + cb * 512
            assert s == 0 and cb < 8
            sl = fill_slot([(w_ada[cb * 2 + kh], 8, 512, 0)])
            sv = slot_view(sl, 8, 512)
            for j in range(4):
                col = s * 48 + cb * 4 + j
                for kk in range(8):
                    st = ada_first[0]
                    ada_first[0] = False
                    op_mm(psum[ADA_BANK][:, col:col + 1], sv[:, kk, j * P:(j + 1) * P], scb[:, kh * 8 + kk:kh * 8 + kk + 1],
                          st, (kh == 1 and kk == 7),
                          reads=[slot_r(sl), ("scb", 0, 1)], writes=[psr(ADA_BANK, col, 1)], skip=True)

        def ada_items(s, cb0, cb1):
            return [(lambda s=s, cb=cb, kh=kh: ada_item(s, cb, kh)) for cb in range(cb0, cb1) for kh in range(2)]

        sbc = rs[:, 0:1024].bitcast(BF16)
        SBC_R = ("rs", 0, 4096)

        def ada_item_dve(t):
            sl = fill_slot([(w_adaD[t], 2, D, 0)])
            sv = slot_view(sl, 2, D)
            for i in range(2):
                col = 32 + 2 * t + i
                S.add("dve", lambda e, o=sv[:, i, :], a=sv[:, i, :], b=sbc, acc=adaT[:, col:col + 1]:
                      e.scalar_tensor_tensor(o, a, 1.0, b, ALU.mult, ALU.mult, accum_out=acc),
                      reads=[slot_r(sl), SBC_R], writes=[slot_r(sl), Bada.r(0, col, col + 1)])

        def ada_fin_a(s):
            if s == 0:
                op_tt(adaT[:, 0:32], psum[ADA_BANK][:, 0:32], pp[:, C_BADA:C_BADA + 32], ALU.add,
                      [psr(ADA_BANK, 0, 32), Bpp.r(0, C_BADA, C_BADA + 32)], [Bada.r(0, 0, 32)])
            else:
                op_tt(adaT[:, s * 48:s * 48 + 32], adaT[:, s * 48:s * 48 + 32], pp[:, C_BADA + s * 48:C_BADA + s * 48 + 32], ALU.add,
                      [Bada.r(0, s * 48, s * 48 + 32), Bpp.r(0, C_BADA + s * 48, C_BADA + s * 48 + 32)], [Bada.r(0, s * 48, s * 48 + 32)])
            gcol = (C_G1, C_GM, C_G2)[s]
            op_stt(gsT[:, s * 16:(s + 1) * 16], adaT[:, s * 48 + 16:s * 48 + 32], 1.0, pp[:, gcol:gcol + 16], ALU.add, ALU.mult,
                   [Bada.r(0, s * 48 + 16, s * 48 + 32), Bpp.r(0, gcol, gcol + 16)], [Bgs.r(0, s * 16, (s + 1) * 16)])

        def ada_fin_b(s):
            op_tt(adaT[:, s * 48 + 32:s * 48 + 48], adaT[:, s * 48 + 32:s * 48 + 48],
                  pp[:, C_BADA + s * 48 + 32:C_BADA + s * 48 + 48], ALU.add,
                  [Bada.r(0, s * 48 + 32, s * 48 + 48), Bpp.r(0, C_BADA + s * 48 + 32, C_BADA + s * 48 + 48)],
                  [Bada.r(0, s * 48 + 32, s * 48 + 48)])
            fac = 1.0 if s == 1 else 0.5
            op_ts(ghT[:, s * 16:(s + 1) * 16], adaT[:, s * 48 + 32:s * 48 + 48], fac, None, ALU.mult, None,
                  [Bada.r(0, s * 48 + 32, s * 48 + 48)], [Bgh.r(0, s * 16, (s + 1) * 16)])

        def tick():
            if not tickq:
                return
            kind, fn = tickq.pop(0)
            fn()
            while tickq and tickq[0][0] == "fin":
                tickq.pop(0)[1]()

        def flush_ticks():
            while tickq:
                tick()

        def blocks_of(a, b):
            out = []
            x = a
            while x < b:
                y = min(x + 512, b)
                out.append((x, y))
                x = y
            return out

        def rms_square(kc, a, b):
            op_act(nT[:, kc, a:b], hT[:, kc, a:b], AF.Square, [Bh.r(kc, a, b)], [Bn.r(kc, a, b)])

        def rms_stats_mm(a, b):
            rms_stats_blocks(blocks_of(a, b))

        def rms_stats_blocks(blks):
            for (x, y) in blks:
                bk = new_bank()
                w = y - x
                for kc in range(KC):
                    op_mm(psum[bk][:, 0:w], ones_rms[:], nT[:, kc, x:y], kc == 0, kc == KC - 1,
                          reads=[("ones_rms", 0, 1), Bn.r(kc, x, y)], writes=[psr(bk, 0, w)])
                op_act(rs[:, x:y], psum[bk][:, 0:w], AF.Sqrt, [psr(bk, 0, w)], [Brs.r(0, x, y)], bias=EPS)
                op_recip(rs[:, x:y], rs[:, x:y], [Brs.r(0, x, y)], [Brs.r(0, x, y)])

        def rms_apply(s, a, b, kcs=None):
            for kc in (range(KC) if kcs is None else kcs):
                tb = kc % 2
                op_tt(Btn.ap3[:, tb, a:b], hT[:, kc, a:b], rs[:, a:b], ALU.mult,
                      [Bh.r(kc, a, b), Brs.r(0, a, b)], [Btn.r(tb, a, b)])
                op_act(nT[:, kc, a:b], Btn.ap3[:, tb, a:b], AF.Identity,
                       [Btn.r(tb, a, b), Bgs.r(0, s * 16 + kc, s * 16 + kc + 1), Bada.r(0, s * 48 + kc, s * 48 + kc + 1)],
                       [Bn.r(kc, a, b)],
                       bias=adaT[:, s * 48 + kc:s * 48 + kc + 1], scale=gsT[:, s * 16 + kc:s * 16 + kc + 1])

        def rms_apply_ps(s, kc, a, b):
            w = b - a
            bk = new_bank()
            op_tt(psum[bk][:, 0:w], hT[:, kc, a:b], rs[:, a:b], ALU.mult,
                  [Bh.r(kc, a, b), Brs.r(0, a, b)], [psr(bk, 0, w)])
            op_act(nT[:, kc, a:b], psum[bk][:, 0:w], AF.Identity,
                   [psr(bk, 0, w), Bgs.r(0, s * 16 + kc, s * 16 + kc + 1), Bada.r(0, s * 48 + kc, s * 48 + kc + 1)],
                   [Bn.r(kc, a, b)],
                   bias=adaT[:, s * 48 + kc:s * 48 + kc + 1], scale=gsT[:, s * 16 + kc:s * 16 + kc + 1])

        def ffn(s, wi, wo, a, b, last_split=None, after_range=None, after_chunk=None):
            tbs = blocks_of(a, b)
            tctr = 0
            for (q0, q1) in ((0, 16), (16, 32), (32, 44)):
                for q in range(q0, q1, 2):
                    sl_h = fill_slot([(wi[q // 2], KC, 256, 0)])
                    sl_u = fill_slot([(wi[22 + q // 2], KC, 256, 0)])
                    vh = slot_view(sl_h, KC, 256)
                    vu = slot_view(sl_u, KC, 256)
                    for j in range(2):
                        pc = q - q0 + j
                        tick()
                        for (x, y) in tbs:
                            w = y - x
                            bh = new_bank()
                            bu = new_bank()
                            for kc in range(KC):
                                op_mm(psum[bh][:, 0:w], vh[:, kc, j * P:(j + 1) * P], nT[:, kc, x:y], kc == 0, kc == KC - 1,
                                      reads=[slot_r(sl_h), Bn.r(kc, x, y)], writes=[psr(bh, 0, w)])
                            for kc in range(KC):
                                op_mm(psum[bu][:, 0:w], vu[:, kc, j * P:(j + 1) * P], nT[:, kc, x:y], kc == 0, kc == KC - 1,
                                      reads=[slot_r(sl_u), Bn.r(kc, x, y)], writes=[psr(bu, 0, w)])
                            ti = tctr % 4
                            tctr += 1
                            op_act(tmp[:, ti, 0:w], psum[bh][:, 0:w], AF.Silu, [psr(bh, 0, w)], [Btmp.r(ti, 0, w)])
                            op_tt(Bp.ap3[:, pc, x:y], tmp[:, ti, 0:w], psum[bu][:, 0:w], ALU.mult,
                                  [Btmp.r(ti, 0, w), psr(bu, 0, w)], [Bp.r(pc, x, y)])
                kn = q1 - q0
                if q1 == NH and last_split is not None:
                    phases = [blocks_of(ra, rb) for (ra, rb) in last_split]
                else:
                    phases = [tbs]
                for pi, ptb in enumerate(phases):
                    for dp in range(8):
                        tick()
                        sl = fill_slot([(wo[dp][:, q0:q1, :], kn, 256, 0)])
                        sv = slot_view(sl, kn, 256)
                        for j in range(2):
                            d = dp * 2 + j
                            if j == 1 and s == 2 and q1 == NH:
                                tick()
                            for (x, y) in ptb:
                                w = y - x
                                bk = new_bank()
                                for kc in range(kn):
                                    op_mm(psum[bk][:, 0:w], sv[:, kc, j * P:(j + 1) * P], Bp.ap3[:, kc, x:y], kc == 0, kc == kn - 1,
                                          reads=[slot_r(sl), Bp.r(kc, x, y)], writes=[psr(bk, 0, w)])
                                op_stt(hT[:, d, x:y], psum[bk][:, 0:w], ghT[:, s * 16 + d:s * 16 + d + 1], hT[:, d, x:y], ALU.mult, ALU.add,
                                       [psr(bk, 0, w), Bgh.r(0, s * 16 + d, s * 16 + d + 1), Bh.r(d, x, y)], [Bh.r(d, x, y)])
                            if q1 == NH and after_chunk is not None:
                                after_chunk(pi, d)
                    if q1 == NH and after_range is not None:
                        after_range(pi)

        def mixer_pass(r0, first, after_d=None):
            o0 = r0 + HALO
            ptbs = [(r0, r0 + 512), (r0 + 512, r0 + MW)]
            for q in range(0, 8, 2):
                sl_a = fill_slot([(w_in[q // 2], KC, 256, 0)])
                sl_g = fill_slot([(w_in[4 + q // 2], KC, 256, 0)])
                va = slot_view(sl_a, KC, 256)
                vg = slot_view(sl_g, KC, 256)
                for j in range(2):
                    c = q + j
                    tick()
                    for (x, y) in ptbs:
                        w = y - x
                        ba = new_bank()
                        bg = new_bank()
                        for kc in range(KC):
                            op_mm(psum[ba][:, 0:w], va[:, kc, j * P:(j + 1) * P], nT[:, kc, x:y], kc == 0, kc == KC - 1,
                                  reads=[slot_r(sl_a), Bn.r(kc, x, y)], writes=[psr(ba, 0, w)])
                        for kc in range(KC):
                            op_mm(psum[bg][:, 0:w], vg[:, kc, j * P:(j + 1) * P], nT[:, kc, x:y], kc == 0, kc == KC - 1,
                                  reads=[slot_r(sl_g), Bn.r(kc, x, y)], writes=[psr(bg, 0, w)])
                        ti = (c * 2 + (0 if x == r0 else 1)) % 4
                        op_act(tmp[:, ti, 0:w], psum[bg][:, 0:w], AF.Sigmoid, [psr(bg, 0, w)], [Btmp.r(ti, 0, w)])
                        op_tt(Babf.ap3[:, c, x - r0:y - r0], psum[ba][:, 0:w], tmp[:, ti, 0:w], ALU.mult,
                              [psr(ba, 0, w), Btmp.r(ti, 0, w)], [Babf.r(c, x - r0, y - r0)])
                    if first:
                        op_ts(Babf.ap3[:, c, 0:HALO], Babf.ap3[:, c, 0:HALO], pp[:, C_HM:C_HM + 1], None, ALU.mult, None,
                              [Babf.r(c, 0, HALO), Bpp.r(0, C_HM, C_HM + 1)], [Babf.r(c, 0, HALO)])
            def build_diag(c):
                Bd = Bdgs[c % 2]
                for k in range(31):
                    sc_ap = pp[:, C_CW + c * 31 + k:C_CW + c * 31 + k + 1]
                    rd = [("identb", 0, 1), Bpp.r(0, C_CW + c * 31 + k, C_CW + c * 31 + k + 1)]
                    if k % 3 != 0:
                        op_ts(Bd.ap3[:, k, :], identb[:], sc_ap, None, ALU.mult, None, rd, [Bd.r(k)])
                    else:
                        op_act(Bd.ap3[:, k, :], identb[:], AF.Identity, rd, [Bd.r(k)], scale=sc_ap)

            build_diag(0)
            for c in range(8):
                tick()
                Bd = Bdgs[c % 2]
                if c + 1 < 8:
                    build_diag(c + 1)
                bk = new_bank()
                for k in range(31):
                    op_mm(psum[bk][:, 0:512], Bd.ap3[:, k, :], Babf.ap3[:, c, 2 + k:2 + k + 512], k == 0, k == 30,
                          reads=[Bd.r(k), Babf.r(c, 2 + k, 2 + k + 512)], writes=[psr(bk)])
                op_act(Bcv.ap3[:, c, :], psum[bk][:, :], AF.Identity, [psr(bk), Bpp.r(0, C_CB + c, C_CB + c + 1)], [Bcv.r(c)],
                       bias=pp[:, C_CB + c:C_CB + c + 1])
                op_act(Bsqm.ap3[:, c, :], psum[bk][:, :], AF.Square, [psr(bk), Bpp.r(0, C_CB + c, C_CB + c + 1)], [Bsqm.r(c)],
                       bias=pp[:, C_CB + c:C_CB + c + 1])
            bm = new_bank()
            bq = new_bank()
            for c in range(8):
                op_mm(psum[bm][:, :], ones32[:], Bcv.ap3[:, c, :], c == 0, c == 7,
                      reads=[("ones32", 0, 1), Bcv.r(c)], writes=[psr(bm)])
            for c in range(8):
                op_mm(psum[bq][:, :], ones_ln[:], Bsqm.ap3[:, c, :], c == 0, c == 7,
                      reads=[("ones_ln", 0, 1), Bsqm.r(c)], writes=[psr(bq)])
            mu = Bln.ap3[:, 0, :]
            var = Bln.ap3[:, 1, :]
            rl = Bln.ap3[:, 2, :]
            op_copy(mu, psum[bm][:, :], [psr(bm)], [Bln.r(0)])
            op_tt(var, mu, mu, ALU.mult, [Bln.r(0)], [Bln.r(1)])
            op_tt(var, psum[bq][:, :], var, ALU.subtract, [psr(bq), Bln.r(1)], [Bln.r(1)])
            op_act(rl, var, AF.Sqrt, [Bln.r(1)], [Bln.r(2)], bias=EPS)
            op_recip(rl, rl, [Bln.r(2)], [Bln.r(2)])
            for c in range(8):
                op_tt(Bcv.ap3[:, c, :], Bcv.ap3[:, c, :], mu, ALU.subtract, [Bcv.r(c), Bln.r(0)], [Bcv.r(c)])
                op_tt(Bcv.ap3[:, c, :], Bcv.ap3[:, c, :], rl, ALU.mult, [Bcv.r(c), Bln.r(2)], [Bcv.r(c)])
                op_act(Bz.ap3[:, c, :], Bcv.ap3[:, c, :], AF.Silu,
                       [Bcv.r(c), Bpp.r(0, C_LG + c, C_LG + c + 1), Bpp.r(0, C_LB + c, C_LB + c + 1)], [Bz.r(c)],
                       bias=pp[:, C_LB + c:C_LB + c + 1], scale=pp[:, C_LG + c:C_LG + c + 1])
            v = Bv.ap3[:, 0, :]
            for q in range(0, 8, 2):
                sl_p = fill_slot([(w_in[8 + q // 2], KC, 256, 0)])
                vp = slot_view(sl_p, KC, 256)
                for j in range(2):
                    c = q + j
                    g = c // 2
                    tick()
                    for (x, y) in ptbs:
                        w = y - x
                        bk = new_bank()
                        for kc in range(KC):
                            op_mm(psum[bk][:, 0:w], vp[:, kc, j * P:(j + 1) * P], nT[:, kc, x:y], kc == 0, kc == KC - 1,
                                  reads=[slot_r(sl_p), Bn.r(kc, x, y)], writes=[psr(bk, 0, w)])
                        op_act(Bv.ap3[:, 0, x - r0:y - r0], psum[bk][:, 0:w], AF.Copy, [psr(bk, 0, w)], [Bv.r(0, x - r0, y - r0)])
                    if first:
                        op_ts(Bv.ap3[:, 0, 0:HALO], Bv.ap3[:, 0, 0:HALO], pp[:, C_HM:C_HM + 1], None, ALU.mult, None,
                              [Bv.r(0, 0, HALO), Bpp.r(0, C_HM, C_HM + 1)], [Bv.r(0, 0, HALO)])
                    cur = 0
                    for lvl in range(g + 1):
                        sh = 1 << lvl
                        nxt = 1 if cur != 1 else 2
                        op_tt(Bv.ap3[:, nxt, 16:MW], Bv.ap3[:, cur, 16:MW], Bv.ap3[:, cur, 16 - sh:MW - sh], ALU.add,
                              [Bv.r(cur, 16 - sh, MW)], [Bv.r(nxt, 16, MW)])
                        cur = nxt
                    op_stt(Bmx.ap3[:, c, :], Bv.ap3[:, cur, HALO:MW], 1.0 / POOL_W[g], Bv.ap3[:, 0, HALO:MW], ALU.mult, ALU.subtract,
                           [Bv.r(cur, HALO, MW), Bv.r(0, HALO, MW)], [Bmx.r(c)])
                    if first:
                        op_tt(Btf.ap3[:, 0, :], Bv.ap3[:, cur, HALO:HALO + 16], pp[:, C_CNT + g * 16:C_CNT + (g + 1) * 16], ALU.mult,
                              [Bv.r(cur, HALO, HALO + 16), Bpp.r(0, C_CNT + g * 16, C_CNT + (g + 1) * 16)], [Btf.r(0)])
                        op_tt(Bmx.ap3[:, c, 0:16], Btf.ap3[:, 0, :], Bv.ap3[:, 0, HALO:HALO + 16], ALU.subtract,
                              [Btf.r(0), Bv.r(0, HALO, HALO + 16)], [Bmx.r(c, 0, 16)])
            for dp in range(8):
                g = dp // 2
                sl_ga = fill_slot([(w_in[12 + dp], KC, 256, 0)])
                sl_gb = fill_slot([(w_in[20 + dp], KC, 256, 0)])
                sl_ab = fill_slot([(w_a_out[dp], 8, 256, 0),
                                   (w_b[dp], 2, 256, 2048)])
                vga = slot_view(sl_ga, KC, 256)
                vgb = slot_view(sl_gb, KC, 256)
                vao = slot_view(sl_ab, 8, 256, 0)
                vbo = slot_view(sl_ab, 2, 256, 2048)
                for j in range(2):
                    d = dp * 2 + j
                    tick()
                    bga = new_bank()
                    bgb = new_bank()
                    bya = new_bank()
                    byb = new_bank()
                    for kc in range(KC):
                        op_mm(psum[bga][:, :], vga[:, kc, j * P:(j + 1) * P], nT[:, kc, o0:o0 + 512], kc == 0, kc == KC - 1,
                              reads=[slot_r(sl_ga), Bn.r(kc, o0, o0 + 512)], writes=[psr(bga)])
                    for kc in range(KC):
                        op_mm(psum[bgb][:, :], vgb[:, kc, j * P:(j + 1) * P], nT[:, kc, o0:o0 + 512], kc == 0, kc == KC - 1,
                              reads=[slot_r(sl_gb), Bn.r(kc, o0, o0 + 512)], writes=[psr(bgb)])
                    for kc in range(8):
                        op_mm(psum[bya][:, :], vao[:, kc, j * P:(j + 1) * P], Bz.ap3[:, kc, :], kc == 0, kc == 7,
                              reads=[slot_r(sl_ab), Bz.r(kc)], writes=[psr(bya)])
                    for kc in range(2):
                        op_mm(psum[byb][:, :], vbo[:, kc, j * P:(j + 1) * P], Bmx.ap3[:, 2 * g + kc, :], kc == 0, kc == 1,
                              reads=[slot_r(sl_ab), Bmx.r(2 * g + kc)], writes=[psr(byb)])
                    op_act(tmp[:, 0, :], psum[bga][:, :], AF.Sigmoid, [psr(bga)], [Btmp.r(0)])
                    op_act(tmp[:, 1, :], psum[bgb][:, :], AF.Sigmoid, [psr(bgb)], [Btmp.r(1)])
                    op_stt(tmp[:, 2, :], psum[bya][:, :], pp[:, C_BA + d:C_BA + d + 1], tmp[:, 0, :], ALU.add, ALU.mult,
                           [psr(bya), Bpp.r(0, C_BA + d, C_BA + d + 1), Btmp.r(0)], [Btmp.r(2)])
                    op_stt(tmp[:, 3, :], psum[byb][:, :], pp[:, C_BB + d:C_BB + d + 1], tmp[:, 1, :], ALU.add, ALU.mult,
                           [psr(byb), Bpp.r(0, C_BB + d, C_BB + d + 1), Btmp.r(1)], [Btmp.r(3)])
                    op_stt(Bm.ap3[:, d, :], tmp[:, 3, :], pp[:, C_LS + d:C_LS + d + 1], tmp[:, 2, :], ALU.mult, ALU.add,
                           [Btmp.r(3), Bpp.r(0, C_LS + d, C_LS + d + 1), Btmp.r(2)], [Bm.r(d)])
            for dp in range(8):
                sl = fill_slot([(w_out[dp], KC, 256, 0)])
                sv = slot_view(sl, KC, 256)
                for j in range(2):
                    d = dp * 2 + j
                    tick()
                    bk = new_bank()
                    for kc in range(KC):
                        op_mm(psum[bk][:, :], sv[:, kc, j * P:(j + 1) * P], Bm.ap3[:, kc, :], kc == 0, kc == KC - 1,
                              reads=[slot_r(sl), Bm.r(kc)], writes=[psr(bk)])
                    op_stt(hT[:, d, o0:o0 + 512], psum[bk][:, :], ghT[:, 16 + d:16 + d + 1], hT[:, d, o0:o0 + 512], ALU.mult, ALU.add,
                           [psr(bk), Bgh.r(0, 16 + d, 16 + d + 1), Bh.r(d, o0, o0 + 512)], [Bh.r(d, o0, o0 + 512)])
                    if after_d is not None:
                        after_d(d)

        for kc in range(KC):
            rms_square(kc, 0, T)
        def ada0_part(g):
            for (c0_, bcol) in ((4 * g, C_BADA + 4 * g), (16 + 4 * g, C_BADA + 16 + 4 * g)):
                op_tt(adaT[:, c0_:c0_ + 4], psum[ADA_BANK][:, c0_:c0_ + 4], pp[:, bcol:bcol + 4], ALU.add,
                      [psr(ADA_BANK, c0_, 4), Bpp.r(0, bcol, bcol + 4)], [Bada.r(0, c0_, c0_ + 4)])
            op_stt(gsT[:, 4 * g:4 * g + 4], adaT[:, 16 + 4 * g:16 + 4 * g + 4], 1.0, pp[:, C_G1 + 4 * g:C_G1 + 4 * g + 4], ALU.add, ALU.mult,
                   [Bada.r(0, 16 + 4 * g, 16 + 4 * g + 4), Bpp.r(0, C_G1 + 4 * g, C_G1 + 4 * g + 4)], [Bgs.r(0, 4 * g, 4 * g + 4)])

        for g_ in range(3):
            for cb_ in (g_, 4 + g_):
                for kh_ in range(2):
                    ada_item(0, cb_, kh_)
        rms_stats_mm(0, T)
        for g_ in range(3):
            ada0_part(g_)
        rms_apply(0, 0, T, kcs=range(0, 12))
        for cb_ in (3, 7):
            for kh_ in range(2):
                ada_item(0, cb_, kh_)
        ada0_part(3)
        rms_apply(0, 0, T, kcs=range(12, 16))
        nbanks[0] = 8
        tmpflat = tmp[:].rearrange("p a b -> p (a b)")
        op_act(sbc, tmpflat, AF.Silu, [Btmp.rr(0, 4)], [SBC_R])
        for t_ in range(56):
            tickq.append(("slot", lambda t_=t_: ada_item_dve(t_)))
            if t_ == 7:
                tickq.append(("fin", lambda: ada_fin_b(0)))
            elif t_ == 31:
                tickq.append(("fin", lambda: (ada_fin_a(1), ada_fin_b(1))))
            elif t_ == 55:
                tickq.append(("fin", lambda: (ada_fin_a(2), ada_fin_b(2))))
        def ffn1_after(pi):
            if pi == 0:
                rms_stats_blocks(blocks_of(0, MW))
                rms_apply(1, 0, MW)

        ffn(0, w1_in, w1_out, 0, T, last_split=[(0, MW), (MW, T)], after_range=ffn1_after,
            after_chunk=lambda pi, d: rms_square(d, *((0, MW), (MW, T))[pi]))
        flush_ticks()
        tickq.append(("slot", lambda: rms_stats_blocks([(MW, T)])))
        for kc in range(KC):
            tickq.append(("slot", lambda kc=kc: rms_apply_ps(1, kc, MW, T)))
        mixer_pass(0, True, after_d=lambda d: rms_square(d, HALO, 512))
        flush_ticks()
        tickq.append(("slot", lambda: rms_stats_blocks([(HALO, 512)])))
        for kc in range(KC):
            tickq.append(("slot", lambda kc=kc: rms_apply_ps(2, kc, HALO, 512)))
        mixer_pass(512, False, after_d=lambda d: rms_square(d, 512, T))
        flush_ticks()
        bl_ = []
        for (x_, y_) in [(512, 1024), (1024, T)]:
            bk_ = new_bank()
            w_ = y_ - x_
            for kc in range(KC):
                op_mm(psum[bk_][:, 0:w_], ones_rms[:], nT[:, kc, x_:y_], kc == 0, kc == KC - 1,
                      reads=[("ones_rms", 0, 1), Bn.r(kc, x_, y_)], writes=[psr(bk_, 0, w_)])
            op_act(psum[bk_][:, 0:w_], psum[bk_][:, 0:w_], AF.Sqrt, [psr(bk_, 0, w_)], [psr(bk_, 0, w_)], bias=EPS)
            op_recip(psum[bk_][:, 0:w_], psum[bk_][:, 0:w_], [psr(bk_, 0, w_)], [psr(bk_, 0, w_)])
            bl_.append((x_, y_, bk_))
        for kc in range(KC):
            tb_ = kc % 2
            for (x_, y_, bk_) in bl_:
                op_tt(Btn.ap3[:, tb_, x_:y_], hT[:, kc, x_:y_], psum[bk_][:, 0:y_ - x_], ALU.mult,
                      [Bh.r(kc, x_, y_), psr(bk_)], [Btn.r(tb_, x_, y_)])
            op_act(nT[:, kc, 512:T], Btn.ap3[:, tb_, 512:T], AF.Identity,
                   [Btn.r(tb_, 512, T), Bgs.r(0, 32 + kc, 32 + kc + 1), Bada.r(0, 96 + kc, 96 + kc + 1)],
                   [Bn.r(kc, 512, T)],
                   bias=adaT[:, 96 + kc:96 + kc + 1], scale=gsT[:, 32 + kc:32 + kc + 1])

        outv = outT.rearrange("(k p) t -> p k t", p=P)

        def final_pre(ra, rb):
            for kc in range(KC):
                rms_square(kc, ra, rb)

        def final_scale(ra, rb, kc, use_pool):
            if kc % 2 == 0 or not use_pool:
                op_stt(hT[:, kc, ra:rb], hT[:, kc, ra:rb], pp[:, C_GF + kc:C_GF + kc + 1], rs[:, ra:rb], ALU.mult, ALU.mult,
                       [Bh.r(kc, ra, rb), Bpp.r(0, C_GF + kc, C_GF + kc + 1), Brs.r(0, ra, rb)], [Bh.r(kc, ra, rb)])
            else:
                op_act(hT[:, kc, ra:rb], hT[:, kc, ra:rb], AF.Identity,
                       [Bh.r(kc, ra, rb), Bpp.r(0, C_GF + kc, C_GF + kc + 1)], [Bh.r(kc, ra, rb)],
                       scale=pp[:, C_GF + kc:C_GF + kc + 1])
                S.add("pool", lambda e, kc=kc: e.tensor_tensor(hT[:, kc, ra:rb], hT[:, kc, ra:rb], rs[:, ra:rb], ALU.mult),
                      reads=[Bh.r(kc, ra, rb), Brs.r(0, ra, rb)], writes=[Bh.r(kc, ra, rb)])
            S.add("sp", lambda e, kc=kc: e.dma_start(out=outv[:, kc, ra - HALO:rb - HALO], in_=hT[:, kc, ra:rb]),
                  reads=[Bh.r(kc, ra, rb)], dma="o")

        def final_post(ra, rb, use_pool):
            rms_stats_blocks(blocks_of(ra, rb))
            for kc in range(KC):
                final_scale(ra, rb, kc, use_pool)

        def ffn2_after(pi):
            if pi == 0:
                tickq.append(("slot", lambda: None))
                tickq.append(("slot", lambda: rms_stats_blocks(blocks_of(HALO, 800))))
                for k2 in range(0, KC, 2):
                    tickq.append(("slot", lambda k2=k2: (final_scale(HALO, 800, k2, False), final_scale(HALO, 800, k2 + 1, False))))
            else:
                flush_ticks()
                final_post(800, T, True)

        ffn(2, w2_in, w2_out, HALO, T, last_split=[(HALO, 800), (800, T)], after_range=ffn2_after,
            after_chunk=lambda pi, d: rms_square(d, *((HALO, 800), (800, T))[pi]))
        n_out = S.dmacnt["o"]

        S.finalize()
        stats = {e: len(S.ops[e]) for e in S.ENGS}

        def run_engine(ename, eng):
            ops = S.ops[ename]
            ms = S.msnum[ename]
            for i, op in enumerate(ops):
                for (key, val) in op["waits"]:
                    if key[0] == "e":
                        eng.wait_ge(sem_e[key[1]], S.msnum[key[1]][val])
                    else:
                        eng.wait_ge(dma_sems[key[1]], val)
                ins = op["emit"](eng)
                if op["dma"] is not None:
                    ins.then_inc(dma_sems[op["dma"]], 16)
                elif op["ms"]:
                    ins.then_inc(sem_e[ename], 1)

        with nc.Block() as block:
            @block.tensor
            def _(e):
                run_engine("pe", e)

            @block.scalar
            def _(e):
                run_engine("act", e)

            @block.vector
            def _(e):
                run_engine("dve", e)

            @block.gpsimd
            def _(e):
                run_engine("pool", e)

            @block.sync
            def _(e):
                run_engine("sp", e)
                e.wait_ge(sem_o, n_out)
    return nc, stats


_CACHE = {}


def _prep_inputs(inp):
    x = np.asarray(inp["x"], dtype=np.float32)
    c = np.asarray(inp["c"], dtype=np.float32)
    b_ada = np.asarray(inp["b_ada"], dtype=np.float32)
    conv_w = np.asarray(inp["conv_w"], dtype=np.float32)

    def fm(v):
        v = np.asarray(v, dtype=np.float32).reshape(-1)
        return np.ascontiguousarray(v.reshape(-1, P).T)

    def tile_cols(w, ncols):
        w = np.asarray(w, dtype=np.float32)
        K, N = w.shape
        return np.ascontiguousarray(w.reshape(K // P, P, N // ncols, ncols).transpose(2, 1, 0, 3))

    wa = np.asarray(inp["w_ada"], dtype=np.float32)
    wa_t = np.ascontiguousarray(wa[:, :4096].reshape(2, 8, P, 8, 512).transpose(3, 0, 2, 1, 4)).reshape(16, P, 8, 512)
    wa_d = np.ascontiguousarray(wa[:, 4096:].T.reshape(56, 2, P, D).transpose(0, 2, 1, 3))
    wb = np.asarray(inp["w_b_group"], dtype=np.float32)
    wb_t = np.ascontiguousarray(wb.reshape(4, 2, P, 2, 256).transpose(0, 3, 2, 1, 4)).reshape(8, P, 2, 256)
    shared = {
        "w_ada": wa_t,
        "w_adaD": wa_d,
        "w1_in": tile_cols(inp["w1_in"], 256),
        "w1_out": tile_cols(inp["w1_out"], 256),
        "w_in": tile_cols(inp["w_in"], 256),
        "w_a_out": tile_cols(inp["w_a_out"], 256),
        "w_b": wb_t,
        "w_out": tile_cols(inp["w_out"], 256),
        "w2_in": tile_cols(inp["w2_in"], 256),
        "w2_out": tile_cols(inp["w2_out"], 256),
    }
    in_maps = []
    for core in range(NCORES):
        b = core // 4
        ch = core % 4
        t0 = ch * TOWN
        xs = np.zeros((T, D), dtype=np.float32)
        if ch > 0:
            xs[:HALO] = x[b, t0 - HALO:t0]
        xs[HALO:] = x[b, t0:t0 + TOWN]
        ppv = np.zeros((P, NPP), dtype=np.float32)
        ppv[:, C_C:C_C + 16] = fm(c[b])
        ppv[:, C_BADA:C_BADA + 144] = fm(b_ada)
        ppv[:, C_G1:C_G1 + 16] = fm(inp["g_ffn1"])
        ppv[:, C_GM:C_GM + 16] = fm(inp["g_mix"])
        ppv[:, C_G2:C_G2 + 16] = fm(inp["g_ffn2"])
        ppv[:, C_GF:C_GF + 16] = fm(inp["g_final"])
        ppv[:, C_CB:C_CB + 8] = fm(inp["conv_b"])
        ppv[:, C_LG:C_LG + 8] = fm(inp["ln_a_g"])
        ppv[:, C_LB:C_LB + 8] = fm(inp["ln_a_b"])
        ppv[:, C_BA:C_BA + 16] = fm(inp["b_a_out"])
        ppv[:, C_BB:C_BB + 16] = fm(np.asarray(inp["b_b_group"]).reshape(-1))
        ppv[:, C_LS:C_LS + 16] = fm(inp["ls_b"])
        cw = conv_w.reshape(31, 8, P).transpose(2, 1, 0).reshape(P, 8 * 31)
        ppv[:, C_CW:C_CW + 248] = cw
        ppv[:, C_HM] = 0.0 if ch == 0 else 1.0
        for g, wdw in enumerate(POOL_W):
            for i in range(16):
                cnt = min(i + 1, wdw) if ch == 0 else wdw
                ppv[:, C_CNT + g * 16 + i] = np.float32(1.0) / np.float32(cnt)
        ppv[:, C_ID:C_ID + P] = np.eye(P, dtype=np.float32)
        m = {"xT": np.ascontiguousarray(xs.T), "pp": ppv,
             "cbc": np.ascontiguousarray(np.broadcast_to(c[b][None, :], (P, D)))}
        m.update(shared)
        in_maps.append(m)
    return in_maps


def kernel(**inputs):
    if "nc" not in _CACHE:
        _CACHE["nc"], _CACHE["stats"] = build_nc()
    nc = _CACHE["nc"]
    in_maps = _prep_inputs(inputs)
    res = run_bass_kernel_spmd(nc, in_maps, core_ids=list(range(NCORES)))
    out = np.empty((2, 4096, D), dtype=np.float32)
    for core in range(NCORES):
        b = core // 4
        ch = core % 4
        out[b, ch * TOWN:(ch + 1) * TOWN, :] = res.results[core]["outT"].T
    return out
```

```python
import numpy as np
import concourse.bass as bass
import concourse.mybir as mybir
from concourse.bass_utils import run_bass_kernel_spmd

F32 = mybir.dt.float32
BF16 = mybir.dt.bfloat16
AF = mybir.ActivationFunctionType
ALU = mybir.AluOpType

P = 128
D = 2048
KC = 16
HALO = 32
TOWN = 1024
T = TOWN + HALO
DFF = 5632
NH = DFF // P
WC = 1024
NCORES = 8
EPS = 1e-6
POOL_W = (2, 4, 8, 16)

C_C = 0
C_BADA = 16
C_G1 = 160
C_GM = 176
C_G2 = 192
C_GF = 208
C_CB = 224
C_LG = 232
C_LB = 240
C_BA = 248
C_BB = 264
C_LS = 280
C_CW = 296
C_HM = 544
C_CNT = 545
C_ID = 609
NPP = 737

NSLOT = 6
SLOT_E = 4096
UW = 10560


class Sched:
    ENGS = ("pe", "act", "dve", "pool", "sp")

    def __init__(self):
        self.ops = {e: [] for e in self.ENGS}
        self.acc = {}
        self.waited = {e: {} for e in self.ENGS}
        self.dmacnt = {}

    def add(self, eng, emit, reads=(), writes=(), dma=None):
        idx = len(self.ops[eng])
        if dma is not None:
            self.dmacnt[dma] = self.dmacnt.get(dma, 0) + 16
            token = ("d", dma, self.dmacnt[dma])
        else:
            token = ("e", eng, idx)
        deps = set()
        for (name, lo, hi) in reads:
            for rec in self.acc.get(name, ()):
                if rec[2] and rec[0] < hi and lo < rec[1]:
                    deps.add(rec[3])
        for (name, lo, hi) in writes:
            for rec in self.acc.get(name, ()):
                if rec[0] < hi and lo < rec[1]:
                    deps.add(rec[3])
        for (name, lo, hi) in writes:
            lst = self.acc.setdefault(name, [])
            lst[:] = [r for r in lst if not (lo <= r[0] and r[1] <= hi)]
            lst.append((lo, hi, True, token))
        for (name, lo, hi) in reads:
            lst = self.acc.setdefault(name, [])
            if token[0] == "e":
                lst[:] = [r for r in lst if not ((not r[2]) and r[3][0] == "e" and r[3][1] == eng
                                                 and lo <= r[0] and r[1] <= hi)]
            lst.append((lo, hi, False, token))
        best = {}
        for tok in deps:
            if tok[0] == "e":
                if tok[1] == "pe" and eng == "pe":
                    continue
                key = ("e", tok[1])
            else:
                if dma is not None and tok[1] == dma:
                    continue
                key = ("d", tok[1])
            if best.get(key, -1) < tok[2]:
                best[key] = tok[2]
        waits = []
        for key, val in best.items():
            if self.waited[eng].get(key, -1) >= val:
                continue
            self.waited[eng][key] = val
            waits.append((key, val))
        self.ops[eng].append({"emit": emit, "waits": waits, "ms": False, "dma": dma})
        return token

    def finalize(self):
        for e in self.ENGS:
            for op in self.ops[e]:
                for (key, val) in op["waits"]:
                    if key[0] == "e":
                        self.ops[key[1]][val]["ms"] = True
        self.msnum = {}
        for e in self.ENGS:
            c = 0
            arr = []
            for op in self.ops[e]:
                if op["ms"]:
                    c += 1
                arr.append(c)
            self.msnum[e] = arr


class Buf:
    def __init__(self, ap3, space, base_bytes, width, esz):
        self.ap3 = ap3
        self.space = space
        self.base = base_bytes
        self.width = width
        self.esz = esz

    def r(self, c, a=0, b=None):
        if b is None:
            b = self.width
        lo = self.base + (c * self.width + a) * self.esz
        hi = self.base + (c * self.width + b) * self.esz
        return (self.space, lo, hi)

    def rr(self, c0, c1):
        return (self.space, self.base + c0 * self.width * self.esz, self.base + c1 * self.width * self.esz)


def build_nc():
    nc = bass.Bass("TRN2", target_bir_lowering=False)
    xT = nc.dram_tensor("xT", [D, T], F32, kind="ExternalInput").ap()
    ppd = nc.dram_tensor("pp", [P, NPP], F32, kind="ExternalInput").ap()
    w_ada = nc.dram_tensor("w_ada", [16, P, 8, 512], F32, kind="ExternalInput").ap()
    w_adaD = nc.dram_tensor("w_adaD", [56, P, 2, D], F32, kind="ExternalInput").ap()
    cbc = nc.dram_tensor("cbc", [P, D], F32, kind="ExternalInput").ap()
    w1_in = nc.dram_tensor("w1_in", [44, P, KC, 256], F32, kind="ExternalInput").ap()
    w1_out = nc.dram_tensor("w1_out", [8, P, NH, 256], F32, kind="ExternalInput").ap()
    w_in = nc.dram_tensor("w_in", [28, P, KC, 256], F32, kind="ExternalInput").ap()
    w_a_out = nc.dram_tensor("w_a_out", [8, P, 8, 256], F32, kind="ExternalInput").ap()
    w_b = nc.dram_tensor("w_b", [8, P, 2, 256], F32, kind="ExternalInput").ap()
    w_out = nc.dram_tensor("w_out", [8, P, KC, 256], F32, kind="ExternalInput").ap()
    w2_in = nc.dram_tensor("w2_in", [44, P, KC, 256], F32, kind="ExternalInput").ap()
    w2_out = nc.dram_tensor("w2_out", [8, P, NH, 256], F32, kind="ExternalInput").ap()
    outT = nc.dram_tensor("outT", [D, TOWN], F32, kind="ExternalOutput").ap()

    from contextlib import ExitStack
    es = ExitStack()
    with es:
        def sb(name, shape, dt):
            return es.enter_context(nc.sbuf_tensor(name, shape, dt))

        hT = sb("hT", [P, KC, T], F32)
        nT = sb("nT", [P, KC, T], BF16)
        ring = sb("ring", [P, NSLOT, SLOT_E], BF16)
        U = sb("U", [P, UW], F32)
        pp = sb("pp_sb", [P, NPP], F32)
        adaT = sb("adaT", [P, 144], F32)
        gsT = sb("gsT", [P, 48], F32)
        ghT = sb("ghT", [P, 48], F32)
        identb = sb("identb", [P, P], BF16)
        ones_rms = sb("ones_rms", [P, P], BF16)
        ones_ln = sb("ones_ln", [P, P], BF16)
        ones32 = sb("ones32", [P, P], F32)
        scb = sb("scb", [P, KC], BF16)
        rs = sb("rs", [P, T], F32)
        tmp = sb("tmp", [P, 4, 512], F32)
        psum = [es.enter_context(nc.psum_tensor("ps%d" % i, [P, 512], F32)) for i in range(8)]

        sem_e = {e: es.enter_context(nc.semaphore("sem_" + e)) for e in ("pe", "act", "dve", "pool")}
        sem_slot = [es.enter_context(nc.semaphore("sem_slot%d" % i)) for i in range(NSLOT)]
        sem_x = [es.enter_context(nc.semaphore("sem_x%d" % i)) for i in range(4)]
        sem_pp = es.enter_context(nc.semaphore("sem_pp"))
        sem_cb = es.enter_context(nc.semaphore("sem_cb"))
        sem_o = es.enter_context(nc.semaphore("sem_o"))
        dma_sems = {"pp": sem_pp, "o": sem_o, "cb": sem_cb}
        for i in range(4):
            dma_sems[("x", i)] = sem_x[i]
        for i in range(NSLOT):
            dma_sems[("slot", i)] = sem_slot[i]

        S = Sched()

        Bh = Buf(hT, "h", 0, T, 4)
        Bn = Buf(nT, "n", 0, T, 2)
        Brs = Buf(None, "rs", 0, T, 4)
        Bpp = Buf(None, "pp", 0, NPP, 4)
        Bada = Buf(None, "ada", 0, 144, 4)
        Bgs = Buf(None, "gs", 0, 48, 4)
        Bgh = Buf(None, "gh", 0, 48, 4)
        Btmp = Buf(tmp, "tmp", 0, 512, 4)

        def uview(w0, nchunk, width, dt):
            esz = 4 if dt == F32 else 2
            nwords = nchunk * width * esz // 4
            assert w0 + nwords <= UW, (w0, nwords)
            ap = U[:, w0:w0 + nwords]
            if dt != F32:
                ap = ap.bitcast(dt)
            ap = ap.rearrange("p (c t) -> p c t", c=nchunk)
            return Buf(ap, "U", w0 * 4, width, esz)

        Bsq = uview(0, KC, T, BF16)
        Bp = Bsq
        Btn = uview(8448, 2, T, F32)
        MW = 544
        Babf = uview(0, 8, MW, BF16)
        Bz = uview(0, 8, 512, BF16)
        Bcv = uview(2176, 8, 512, F32)
        Bdg = uview(6272, 31, 128, BF16)
        _dg1 = tmp[:].rearrange("p a b -> p (a b)")[:, 0:1984].bitcast(BF16).rearrange("p (c t) -> p c t", c=31)
        Bdg1 = Buf(_dg1, "tmp", 0, 128, 2)
        Bdgs = (Bdg, Bdg1)
        Bsqm = uview(8320, 8, 512, BF16)
        Bm = uview(6272, 16, 512, BF16)
        Bln = uview(6272, 3, 512, F32)
        Bv = uview(2176, 3, MW, F32)
        Btf = uview(3808, 1, 16, F32)
        Bmx = uview(4096, 8, 512, BF16)

        bank_ctr = [0]

        ADA_BANK = 7
        nbanks = [7]

        def new_bank():
            b = bank_ctr[0] % nbanks[0]
            bank_ctr[0] += 1
            return b

        def psr(b, a=0, w=512):
            return (("ps", b), 0, 2048)

        def op_mm(out_ap, lhsT, rhs, start, stop, reads, writes, skip=False):
            S.add("pe", lambda e, o=out_ap, l=lhsT, r=rhs, st=start, sp=stop, sk=skip:
                  e.matmul(o, l, r, start=st, stop=sp, skip_group_check=sk),
                  reads=reads, writes=writes)

        def op_act(out_ap, in_ap, func, reads, writes, bias=None, scale=None):
            def emit(e, o=out_ap, i=in_ap, f=func, b=bias, s=scale):
                kw = {}
                if b is not None:
                    kw["bias"] = b
                if s is not None:
                    kw["scale"] = s
                return e.activation(o, i, f, **kw)
            S.add("act", emit, reads=reads, writes=writes)

        def op_tt(out_ap, in0, in1, op, reads, writes):
            S.add("dve", lambda e, o=out_ap, a=in0, b=in1, p=op: e.tensor_tensor(o, a, b, p), reads=reads, writes=writes)

        def op_ts(out_ap, in0, s1, s2, op0, op1, reads, writes):
            def emit(e, o=out_ap, a=in0, x=s1, y=s2, p0=op0, p1=op1):
                if p1 is None:
                    return e.tensor_scalar(o, a, x, None, p0)
                return e.tensor_scalar(o, a, x, y, p0, p1)
            S.add("dve", emit, reads=reads, writes=writes)

        def op_stt(out_ap, in0, scalar, in1, op0, op1, reads, writes):
            S.add("dve", lambda e, o=out_ap, a=in0, s=scalar, b=in1, p0=op0, p1=op1:
                  e.scalar_tensor_tensor(o, a, s, b, p0, p1), reads=reads, writes=writes)

        def op_copy(out_ap, in_ap, reads, writes):
            S.add("dve", lambda e, o=out_ap, i=in_ap: e.tensor_copy(o, i), reads=reads, writes=writes)

        def op_recip(out_ap, in_ap, reads, writes):
            S.add("dve", lambda e, o=out_ap, i=in_ap: e.reciprocal(o, i), reads=reads, writes=writes)

        def op_memset(ap, val, writes):
            S.add("dve", lambda e, a=ap, v=val: e.memset(a, v), writes=writes)

        slot_ctr = [0]

        def fill_slot(pieces):
            s = slot_ctr[0] % NSLOT
            slot_ctr[0] += 1
            for (src, kn, ncols, eoff) in pieces:
                dst = ring[:, s, eoff:eoff + kn * ncols].rearrange("p (k c) -> p k c", k=kn)
                srcv = src
                S.add("pool", lambda e, o=dst, i=srcv: e.dma_start(out=o, in_=i, max_dma_last_dim=8192),
                      writes=[(("ring", s), eoff * 2, (eoff + kn * ncols) * 2)], dma=("slot", s))
            return s

        def slot_view(s, kn, ncols, eoff=0):
            return ring[:, s, eoff:eoff + kn * ncols].rearrange("p (k c) -> p k c", k=kn)

        def slot_r(s):
            return (("ring", s), 0, SLOT_E * 2)

        S.add("sp", lambda e: e.dma_start(out=pp[:], in_=ppd), writes=[Bpp.r(0)], dma="pp")
        for g4 in range(4):
            S.add("sp", lambda e, g4=g4: e.dma_start(
                out=hT[:, g4 * 4:(g4 + 1) * 4, :],
                in_=xT[g4 * 512:(g4 + 1) * 512, :].rearrange("(k p) t -> p k t", p=P)),
                writes=[Bh.rr(g4 * 4, g4 * 4 + 4)], dma=("x", g4))
        S.add("sp", lambda e: e.dma_start(out=tmp[:].rearrange("p a b -> p (a b)"), in_=cbc), writes=[Btmp.rr(0, 4)], dma="cb")
        op_memset(ones_rms[:], 1.0 / D, [("ones_rms", 0, 1)])
        op_memset(ones_ln[:], 1.0 / WC, [("ones_ln", 0, 1)])
        op_memset(ones32[:], 1.0 / WC, [("ones32", 0, 1)])
        op_copy(identb[:], pp[:, C_ID:C_ID + P], [Bpp.r(0, C_ID, C_ID + P)], [("identb", 0, 1)])
        op_act(scb[:], pp[:, C_C:C_C + KC], AF.Silu, [Bpp.r(0, C_C, C_C + KC)], [("scb", 0, 1)])

        ada_first = [True]
        tickq = []

        def ada_item(s, cb, kh):
            c0 = s * 48 * P + cb * 512
            assert s == 0 and cb < 8
            sl = fill_slot([(w_ada[cb * 2 + kh], 8, 512, 0)])
            sv = slot_view(sl, 8, 512)
            for j in range(4):
                col = s * 48 + cb * 4 + j
                for kk in range(8):
                    st = ada_first[0]
                    ada_first[0] = False
                    op_mm(psum[ADA_BANK][:, col:col + 1], sv[:, kk, j * P:(j + 1) * P], scb[:, kh * 8 + kk:kh * 8 + kk + 1],
                          st, (kh == 1 and kk == 7),
                          reads=[slot_r(sl), ("scb", 0, 1)], writes=[psr(ADA_BANK, col, 1)], skip=True)

        def ada_items(s, cb0, cb1):
            return [(lambda s=s, cb=cb, kh=kh: ada_item(s, cb, kh)) for cb in range(cb0, cb1) for kh in range(2)]

        sbc = rs[:, 0:1024].bitcast(BF16)
        SBC_R = ("rs", 0, 4096)

        def ada_item_dve(t):
            sl = fill_slot([(w_adaD[t], 2, D, 0)])
            sv = slot_view(sl, 2, D)
            for i in range(2):
                col = 32 + 2 * t + i
                S.add("dve", lambda e, o=sv[:, i, :], a=sv[:, i, :], b=sbc, acc=adaT[:, col:col + 1]:
                      e.scalar_tensor_tensor(o, a, 1.0, b, ALU.mult, ALU.mult, accum_out=acc),
                      reads=[slot_r(sl), SBC_R], writes=[slot_r(sl), Bada.r(0, col, col + 1)])

        def ada_fin_a(s):
            if s == 0:
                op_tt(adaT[:, 0:32], psum[ADA_BANK][:, 0:32], pp[:, C_BADA:C_BADA + 32], ALU.add,
                      [psr(ADA_BANK, 0, 32), Bpp.r(0, C_BADA, C_BADA + 32)], [Bada.r(0, 0, 32)])
            else:
                op_tt(adaT[:, s * 48:s * 48 + 32], adaT[:, s * 48:s * 48 + 32], pp[:, C_BADA + s * 48:C_BADA + s * 48 + 32], ALU.add,
                      [Bada.r(0, s * 48, s * 48 + 32), Bpp.r(0, C_BADA + s * 48, C_BADA + s * 48 + 32)], [Bada.r(0, s * 48, s * 48 + 32)])
            gcol = (C_G1, C_GM, C_G2)[s]
            op_stt(gsT[:, s * 16:(s + 1) * 16], adaT[:, s * 48 + 16:s * 48 + 32], 1.0, pp[:, gcol:gcol + 16], ALU.add, ALU.mult,
                   [Bada.r(0, s * 48 + 16, s * 48 + 32), Bpp.r(0, gcol, gcol + 16)], [Bgs.r(0, s * 16, (s + 1) * 16)])

        def ada_fin_b(s):
            op_tt(adaT[:, s * 48 + 32:s * 48 + 48], adaT[:, s * 48 + 32:s * 48 + 48],
                  pp[:, C_BADA + s * 48 + 32:C_BADA + s * 48 + 48], ALU.add,
                  [Bada.r(0, s * 48 + 32, s * 48 + 48), Bpp.r(0, C_BADA + s * 48 + 32, C_BADA + s * 48 + 48)],
                  [Bada.r(0, s * 48 + 32, s * 48 + 48)])
            fac = 1.0 if s == 1 else 0.5
            op_ts(ghT[:, s * 16:(s + 1) * 16], adaT[:, s * 48 + 32:s * 48 + 48], fac, None, ALU.mult, None,
                  [Bada.r(0, s * 48 + 32, s * 48 + 48)], [Bgh.r(0, s * 16, (s + 1) * 16)])

        def tick():
            if not tickq:
                return
            kind, fn = tickq.pop(0)
            fn()
            while tickq and tickq[0][0] == "fin":
                tickq.pop(0)[1]()

        def flush_ticks():
            while tickq:
                tick()

        def blocks_of(a, b):
            out = []
            x = a
            while x < b:
                y = min(x + 512, b)
                out.append((x, y))
                x = y
            return out

        def rms_square(kc, a, b):
            op_act(nT[:, kc, a:b], hT[:, kc, a:b], AF.Square, [Bh.r(kc, a, b)], [Bn.r(kc, a, b)])

        def rms_stats_mm(a, b):
            rms_stats_blocks(blocks_of(a, b))

        def rms_stats_blocks(blks):
            for (x, y) in blks:
                bk = new_bank()
                w = y - x
                for kc in range(KC):
                    op_mm(psum[bk][:, 0:w], ones_rms[:], nT[:, kc, x:y], kc == 0, kc == KC - 1,
                          reads=[("ones_rms", 0, 1), Bn.r(kc, x, y)], writes=[psr(bk, 0, w)])
                op_act(rs[:, x:y], psum[bk][:, 0:w], AF.Sqrt, [psr(bk, 0, w)], [Brs.r(0, x, y)], bias=EPS)
                op_recip(rs[:, x:y], rs[:, x:y], [Brs.r(0, x, y)], [Brs.r(0, x, y)])

        def rms_apply(s, a, b, kcs=None):
            for kc in (range(KC) if kcs is None else kcs):
                tb = kc % 2
                op_tt(Btn.ap3[:, tb, a:b], hT[:, kc, a:b], rs[:, a:b], ALU.mult,
                      [Bh.r(kc, a, b), Brs.r(0, a, b)], [Btn.r(tb, a, b)])
                op_act(nT[:, kc, a:b], Btn.ap3[:, tb, a:b], AF.Identity,
                       [Btn.r(tb, a, b), Bgs.r(0, s * 16 + kc, s * 16 + kc + 1), Bada.r(0, s * 48 + kc, s * 48 + kc + 1)],
                       [Bn.r(kc, a, b)],
                       bias=adaT[:, s * 48 + kc:s * 48 + kc + 1], scale=gsT[:, s * 16 + kc:s * 16 + kc + 1])

        def rms_apply_ps(s, kc, a, b):
            w = b - a
            bk = new_bank()
            op_tt(psum[bk][:, 0:w], hT[:, kc, a:b], rs[:, a:b], ALU.mult,
                  [Bh.r(kc, a, b), Brs.r(0, a, b)], [psr(bk, 0, w)])
            op_act(nT[:, kc, a:b], psum[bk][:, 0:w], AF.Identity,
                   [psr(bk, 0, w), Bgs.r(0, s * 16 + kc, s * 16 + kc + 1), Bada.r(0, s * 48 + kc, s * 48 + kc + 1)],
                   [Bn.r(kc, a, b)],
                   bias=adaT[:, s * 48 + kc:s * 48 + kc + 1], scale=gsT[:, s * 16 + kc:s * 16 + kc + 1])

        def ffn(s, wi, wo, a, b, last_split=None, after_range=None, after_chunk=None):
            tbs = blocks_of(a, b)
            if b - a == T:
                tbs = [(0, 496), (496, 992), (992, T)]
            tctr = 0
            for (q0, q1) in ((0, 16), (16, 32), (32, 44)):
                for q in range(q0, q1, 2):
                    sl_h = fill_slot([(wi[q // 2], KC, 256, 0)])
                    sl_u = fill_slot([(wi[22 + q // 2], KC, 256, 0)])
                    vh = slot_view(sl_h, KC, 256)
                    vu = slot_view(sl_u, KC, 256)
                    for j in range(2):
                        pc = q - q0 + j
                        tick()
                        for (x, y) in tbs:
                            w = y - x
                            bh = new_bank()
                            bu = new_bank()
                            for kc in range(KC):
                                op_mm(psum[bh][:, 0:w], vh[:, kc, j * P:(j + 1) * P], nT[:, kc, x:y], kc == 0, kc == KC - 1,
                                      reads=[slot_r(sl_h), Bn.r(kc, x, y)], writes=[psr(bh, 0, w)])
                            for kc in range(KC):
                                op_mm(psum[bu][:, 0:w], vu[:, kc, j * P:(j + 1) * P], nT[:, kc, x:y], kc == 0, kc == KC - 1,
                                      reads=[slot_r(sl_u), Bn.r(kc, x, y)], writes=[psr(bu, 0, w)])
                            ti = tctr % 4
                            tctr += 1
                            op_act(tmp[:, ti, 0:w], psum[bh][:, 0:w], AF.Silu, [psr(bh, 0, w)], [Btmp.r(ti, 0, w)])
                            op_tt(Bp.ap3[:, pc, x:y], tmp[:, ti, 0:w], psum[bu][:, 0:w], ALU.mult,
                                  [Btmp.r(ti, 0, w), psr(bu, 0, w)], [Bp.r(pc, x, y)])
                kn = q1 - q0
                if q1 == NH and last_split is not None:
                    phases = [blocks_of(ra, rb) for (ra, rb) in last_split]
                else:
                    phases = [tbs]
                for pi, ptb in enumerate(phases):
                    for dp in range(8):
                        tick()
                        sl = fill_slot([(wo[dp][:, q0:q1, :], kn, 256, 0)])
                        sv = slot_view(sl, kn, 256)
                        for j in range(2):
                            d = dp * 2 + j
                            if j == 1 and s == 2 and q1 == NH:
                                tick()
                            for (x, y) in ptb:
                                w = y - x
                                bk = new_bank()
                                for kc in range(kn):
                                    op_mm(psum[bk][:, 0:w], sv[:, kc, j * P:(j + 1) * P], Bp.ap3[:, kc, x:y], kc == 0, kc == kn - 1,
                                          reads=[slot_r(sl), Bp.r(kc, x, y)], writes=[psr(bk, 0, w)])
                                op_stt(hT[:, d, x:y], psum[bk][:, 0:w], ghT[:, s * 16 + d:s * 16 + d + 1], hT[:, d, x:y], ALU.mult, ALU.add,
                                       [psr(bk, 0, w), Bgh.r(0, s * 16 + d, s * 16 + d + 1), Bh.r(d, x, y)], [Bh.r(d, x, y)])
                            if q1 == NH and after_chunk is not None:
                                after_chunk(pi, d)
                    if q1 == NH and after_range is not None:
                        after_range(pi)

        def mixer_pass(r0, first, after_d=None):
            o0 = r0 + HALO
            ptbs = [(r0, r0 + 512), (r0 + 512, r0 + MW)]
            for q in range(0, 8, 2):
                sl_a = fill_slot([(w_in[q // 2], KC, 256, 0)])
                sl_g = fill_slot([(w_in[4 + q // 2], KC, 256, 0)])
                va = slot_view(sl_a, KC, 256)
                vg = slot_view(sl_g, KC, 256)
                for j in range(2):
                    c = q + j
                    tick()
                    for (x, y) in ptbs:
                        w = y - x
                        ba = new_bank()
                        bg = new_bank()
                        for kc in range(KC):
                            op_mm(psum[ba][:, 0:w], va[:, kc, j * P:(j + 1) * P], nT[:, kc, x:y], kc == 0, kc == KC - 1,
                                  reads=[slot_r(sl_a), Bn.r(kc, x, y)], writes=[psr(ba, 0, w)])
                        for kc in range(KC):
                            op_mm(psum[bg][:, 0:w], vg[:, kc, j * P:(j + 1) * P], nT[:, kc, x:y], kc == 0, kc == KC - 1,
                                  reads=[slot_r(sl_g), Bn.r(kc, x, y)], writes=[psr(bg, 0, w)])
                        ti = (c * 2 + (0 if x == r0 else 1)) % 4
                        op_act(tmp[:, ti, 0:w], psum[bg][:, 0:w], AF.Sigmoid, [psr(bg, 0, w)], [Btmp.r(ti, 0, w)])
                        op_tt(Babf.ap3[:, c, x - r0:y - r0], psum[ba][:, 0:w], tmp[:, ti, 0:w], ALU.mult,
                              [psr(ba, 0, w), Btmp.r(ti, 0, w)], [Babf.r(c, x - r0, y - r0)])
                    if first:
                        op_ts(Babf.ap3[:, c, 0:HALO], Babf.ap3[:, c, 0:HALO], pp[:, C_HM:C_HM + 1], None, ALU.mult, None,
                              [Babf.r(c, 0, HALO), Bpp.r(0, C_HM, C_HM + 1)], [Babf.r(c, 0, HALO)])
            def build_diag(c):
                Bd = Bdgs[c % 2]
                for k in range(31):
                    sc_ap = pp[:, C_CW + c * 31 + k:C_CW + c * 31 + k + 1]
                    rd = [("identb", 0, 1), Bpp.r(0, C_CW + c * 31 + k, C_CW + c * 31 + k + 1)]
                    if k % 3 != 0:
                        op_ts(Bd.ap3[:, k, :], identb[:], sc_ap, None, ALU.mult, None, rd, [Bd.r(k)])
                    else:
                        op_act(Bd.ap3[:, k, :], identb[:], AF.Identity, rd, [Bd.r(k)], scale=sc_ap)

            build_diag(0)
            for c in range(8):
                tick()
                Bd = Bdgs[c % 2]
                if c + 1 < 8:
                    build_diag(c + 1)
                bk = new_bank()
                for k in range(31):
                    op_mm(psum[bk][:, 0:512], Bd.ap3[:, k, :], Babf.ap3[:, c, 2 + k:2 + k + 512], k == 0, k == 30,
                          reads=[Bd.r(k), Babf.r(c, 2 + k, 2 + k + 512)], writes=[psr(bk)])
                op_act(Bcv.ap3[:, c, :], psum[bk][:, :], AF.Identity, [psr(bk), Bpp.r(0, C_CB + c, C_CB + c + 1)], [Bcv.r(c)],
                       bias=pp[:, C_CB + c:C_CB + c + 1])
                op_act(Bsqm.ap3[:, c, :], psum[bk][:, :], AF.Square, [psr(bk), Bpp.r(0, C_CB + c, C_CB + c + 1)], [Bsqm.r(c)],
                       bias=pp[:, C_CB + c:C_CB + c + 1])
            bm = new_bank()
            bq = new_bank()
            for c in range(8):
                op_mm(psum[bm][:, :], ones32[:], Bcv.ap3[:, c, :], c == 0, c == 7,
                      reads=[("ones32", 0, 1), Bcv.r(c)], writes=[psr(bm)])
            for c in range(8):
                op_mm(psum[bq][:, :], ones_ln[:], Bsqm.ap3[:, c, :], c == 0, c == 7,
                      reads=[("ones_ln", 0, 1), Bsqm.r(c)], writes=[psr(bq)])
            mu = Bln.ap3[:, 0, :]
            var = Bln.ap3[:, 1, :]
            rl = Bln.ap3[:, 2, :]
            op_copy(mu, psum[bm][:, :], [psr(bm)], [Bln.r(0)])
            op_tt(var, mu, mu, ALU.mult, [Bln.r(0)], [Bln.r(1)])
            op_tt(var, psum[bq][:, :], var, ALU.subtract, [psr(bq), Bln.r(1)], [Bln.r(1)])
            op_act(rl, var, AF.Sqrt, [Bln.r(1)], [Bln.r(2)], bias=EPS)
            op_recip(rl, rl, [Bln.r(2)], [Bln.r(2)])
            for c in range(8):
                op_tt(Bcv.ap3[:, c, :], Bcv.ap3[:, c, :], mu, ALU.subtract, [Bcv.r(c), Bln.r(0)], [Bcv.r(c)])
                op_tt(Bcv.ap3[:, c, :], Bcv.ap3[:, c, :], rl, ALU.mult, [Bcv.r(c), Bln.r(2)], [Bcv.r(c)])
                op_act(Bz.ap3[:, c, :], Bcv.ap3[:, c, :], AF.Silu,
                       [Bcv.r(c), Bpp.r(0, C_LG + c, C_LG + c + 1), Bpp.r(0, C_LB + c, C_LB + c + 1)], [Bz.r(c)],
                       bias=pp[:, C_LB + c:C_LB + c + 1], scale=pp[:, C_LG + c:C_LG + c + 1])
            v = Bv.ap3[:, 0, :]
            for q in range(0, 8, 2):
                sl_p = fill_slot([(w_in[8 + q // 2], KC, 256, 0)])
                vp = slot_view(sl_p, KC, 256)
                for j in range(2):
                    c = q + j
                    g = c // 2
                    tick()
                    for (x, y) in ptbs:
                        w = y - x
                        bk = new_bank()
                        for kc in range(KC):
                            op_mm(psum[bk][:, 0:w], vp[:, kc, j * P:(j + 1) * P], nT[:, kc, x:y], kc == 0, kc == KC - 1,
                                  reads=[slot_r(sl_p), Bn.r(kc, x, y)], writes=[psr(bk, 0, w)])
                        op_act(Bv.ap3[:, 0, x - r0:y - r0], psum[bk][:, 0:w], AF.Copy, [psr(bk, 0, w)], [Bv.r(0, x - r0, y - r0)])
                    if first:
                        op_ts(Bv.ap3[:, 0, 0:HALO], Bv.ap3[:, 0, 0:HALO], pp[:, C_HM:C_HM + 1], None, ALU.mult, None,
                              [Bv.r(0, 0, HALO), Bpp.r(0, C_HM, C_HM + 1)], [Bv.r(0, 0, HALO)])
                    cur = 0
                    for lvl in range(g + 1):
                        sh = 1 << lvl
                        nxt = 1 if cur != 1 else 2
                        op_tt(Bv.ap3[:, nxt, 16:MW], Bv.ap3[:, cur, 16:MW], Bv.ap3[:, cur, 16 - sh:MW - sh], ALU.add,
                              [Bv.r(cur, 16 - sh, MW)], [Bv.r(nxt, 16, MW)])
                        cur = nxt
                    op_stt(Bmx.ap3[:, c, :], Bv.ap3[:, cur, HALO:MW], 1.0 / POOL_W[g], Bv.ap3[:, 0, HALO:MW], ALU.mult, ALU.subtract,
                           [Bv.r(cur, HALO, MW), Bv.r(0, HALO, MW)], [Bmx.r(c)])
                    if first:
                        op_tt(Btf.ap3[:, 0, :], Bv.ap3[:, cur, HALO:HALO + 16], pp[:, C_CNT + g * 16:C_CNT + (g + 1) * 16], ALU.mult,
                              [Bv.r(cur, HALO, HALO + 16), Bpp.r(0, C_CNT + g * 16, C_CNT + (g + 1) * 16)], [Btf.r(0)])
                        op_tt(Bmx.ap3[:, c, 0:16], Btf.ap3[:, 0, :], Bv.ap3[:, 0, HALO:HALO + 16], ALU.subtract,
                              [Btf.r(0), Bv.r(0, HALO, HALO + 16)], [Bmx.r(c, 0, 16)])
            for dp in range(8):
                g = dp // 2
                sl_ga = fill_slot([(w_in[12 + dp], KC, 256, 0)])
                sl_gb = fill_slot([(w_in[20 + dp], KC, 256, 0)])
                sl_ab = fill_slot([(w_a_out[dp], 8, 256, 0),
                                   (w_b[dp], 2, 256, 2048)])
                vga = slot_view(sl_ga, KC, 256)
                vgb = slot_view(sl_gb, KC, 256)
                vao = slot_view(sl_ab, 8, 256, 0)
                vbo = slot_view(sl_ab, 2, 256, 2048)
                for j in range(2):
                    d = dp * 2 + j
                    tick()
                    bga = new_bank()
                    bgb = new_bank()
                    bya = new_bank()
                    byb = new_bank()
                    for kc in range(KC):
                        op_mm(psum[bga][:, :], vga[:, kc, j * P:(j + 1) * P], nT[:, kc, o0:o0 + 512], kc == 0, kc == KC - 1,
                              reads=[slot_r(sl_ga), Bn.r(kc, o0, o0 + 512)], writes=[psr(bga)])
                    for kc in range(KC):
                        op_mm(psum[bgb][:, :], vgb[:, kc, j * P:(j + 1) * P], nT[:, kc, o0:o0 + 512], kc == 0, kc == KC - 1,
                              reads=[slot_r(sl_gb), Bn.r(kc, o0, o0 + 512)], writes=[psr(bgb)])
                    for kc in range(8):
                        op_mm(psum[bya][:, :], vao[:, kc, j * P:(j + 1) * P], Bz.ap3[:, kc, :], kc == 0, kc == 7,
                              reads=[slot_r(sl_ab), Bz.r(kc)], writes=[psr(bya)])
                    for kc in range(2):
                        op_mm(psum[byb][:, :], vbo[:, kc, j * P:(j + 1) * P], Bmx.ap3[:, 2 * g + kc, :], kc == 0, kc == 1,
                              reads=[slot_r(sl_ab), Bmx.r(2 * g + kc)], writes=[psr(byb)])
                    op_act(tmp[:, 0, :], psum[bga][:, :], AF.Sigmoid, [psr(bga)], [Btmp.r(0)])
                    op_act(tmp[:, 1, :], psum[bgb][:, :], AF.Sigmoid, [psr(bgb)], [Btmp.r(1)])
                    op_stt(tmp[:, 2, :], psum[bya][:, :], pp[:, C_BA + d:C_BA + d + 1], tmp[:, 0, :], ALU.add, ALU.mult,
                           [psr(bya), Bpp.r(0, C_BA + d, C_BA + d + 1), Btmp.r(0)], [Btmp.r(2)])
                    op_stt(tmp[:, 3, :], psum[byb][:, :], pp[:, C_BB + d:C_BB + d + 1], tmp[:, 1, :], ALU.add, ALU.mult,
                           [psr(byb), Bpp.r(0, C_BB + d, C_BB + d + 1), Btmp.r(1)], [Btmp.r(3)])
                    op_stt(Bm.ap3[:, d, :], tmp[:, 3, :], pp[:, C_LS + d:C_LS + d + 1], tmp[:, 2, :], ALU.mult, ALU.add,
                           [Btmp.r(3), Bpp.r(0, C_LS + d, C_LS + d + 1), Btmp.r(2)], [Bm.r(d)])
            for dp in range(8):
                sl = fill_slot([(w_out[dp], KC, 256, 0)])
                sv = slot_view(sl, KC, 256)
                for j in range(2):
                    d = dp * 2 + j
                    tick()
                    bk = new_bank()
                    for kc in range(KC):
                        op_mm(psum[bk][:, :], sv[:, kc, j * P:(j + 1) * P], Bm.ap3[:, kc, :], kc == 0, kc == KC - 1,
                              reads=[slot_r(sl), Bm.r(kc)], writes=[psr(bk)])
                    op_stt(hT[:, d, o0:o0 + 512], psum[bk][:, :], ghT[:, 16 + d:16 + d + 1], hT[:, d, o0:o0 + 512], ALU.mult, ALU.add,
                           [psr(bk), Bgh.r(0, 16 + d, 16 + d + 1), Bh.r(d, o0, o0 + 512)], [Bh.r(d, o0, o0 + 512)])
                    if after_d is not None:
                        after_d(d)

        for kc in range(KC):
            rms_square(kc, 0, T)
        def ada0_part(g):
            for (c0_, bcol) in ((4 * g, C_BADA + 4 * g), (16 + 4 * g, C_BADA + 16 + 4 * g)):
                op_tt(adaT[:, c0_:c0_ + 4], psum[ADA_BANK][:, c0_:c0_ + 4], pp[:, bcol:bcol + 4], ALU.add,
                      [psr(ADA_BANK, c0_, 4), Bpp.r(0, bcol, bcol + 4)], [Bada.r(0, c0_, c0_ + 4)])
            op_stt(gsT[:, 4 * g:4 * g + 4], adaT[:, 16 + 4 * g:16 + 4 * g + 4], 1.0, pp[:, C_G1 + 4 * g:C_G1 + 4 * g + 4], ALU.add, ALU.mult,
                   [Bada.r(0, 16 + 4 * g, 16 + 4 * g + 4), Bpp.r(0, C_G1 + 4 * g, C_G1 + 4 * g + 4)], [Bgs.r(0, 4 * g, 4 * g + 4)])

        for g_ in range(3):
            for cb_ in (g_, 4 + g_):
                for kh_ in range(2):
                    ada_item(0, cb_, kh_)
        rms_stats_mm(0, T)
        for g_ in range(3):
            ada0_part(g_)
        rms_apply(0, 0, T, kcs=range(0, 12))
        for cb_ in (3, 7):
            for kh_ in range(2):
                ada_item(0, cb_, kh_)
        ada0_part(3)
        rms_apply(0, 0, T, kcs=range(12, 16))
        nbanks[0] = 8
        tmpflat = tmp[:].rearrange("p a b -> p (a b)")
        op_act(sbc, tmpflat, AF.Silu, [Btmp.rr(0, 4)], [SBC_R])
        for t_ in range(56):
            tickq.append(("slot", lambda t_=t_: ada_item_dve(t_)))
            if t_ == 7:
                tickq.append(("fin", lambda: ada_fin_b(0)))
            elif t_ == 31:
                tickq.append(("fin", lambda: (ada_fin_a(1), ada_fin_b(1))))
            elif t_ == 55:
                tickq.append(("fin", lambda: (ada_fin_a(2), ada_fin_b(2))))
        def ffn1_after(pi):
            if pi == 0:
                rms_stats_blocks(blocks_of(0, MW))
                rms_apply(1, 0, MW)

        ffn(0, w1_in, w1_out, 0, T, last_split=[(0, MW), (MW, T)], after_range=ffn1_after,
            after_chunk=lambda pi, d: rms_square(d, *((0, MW), (MW, T))[pi]))
        flush_ticks()
        tickq.append(("slot", lambda: rms_stats_blocks([(MW, T)])))
        for kc in range(KC):
            tickq.append(("slot", lambda kc=kc: rms_apply_ps(1, kc, MW, T)))
        mixer_pass(0, True, after_d=lambda d: rms_square(d, HALO, 512))
        flush_ticks()
        tickq.append(("slot", lambda: rms_stats_blocks([(HALO, 512)])))
        for kc in range(KC):
            tickq.append(("slot", lambda kc=kc: rms_apply_ps(2, kc, HALO, 512)))
        mixer_pass(512, False, after_d=lambda d: rms_square(d, 512, T))
        flush_ticks()
        rms_stats_blocks([(512, 1024), (1024, T)])
        rms_apply(2, 512, T)

        outv = outT.rearrange("(k p) t -> p k t", p=P)

        def final_pre(ra, rb):
            for kc in range(KC):
                rms_square(kc, ra, rb)

        def final_scale(ra, rb, kc, use_pool):
            if kc % 2 == 0 or not use_pool:
                op_stt(hT[:, kc, ra:rb], hT[:, kc, ra:rb], pp[:, C_GF + kc:C_GF + kc + 1], rs[:, ra:rb], ALU.mult, ALU.mult,
                       [Bh.r(kc, ra, rb), Bpp.r(0, C_GF + kc, C_GF + kc + 1), Brs.r(0, ra, rb)], [Bh.r(kc, ra, rb)])
            else:
                op_act(hT[:, kc, ra:rb], hT[:, kc, ra:rb], AF.Identity,
                       [Bh.r(kc, ra, rb), Bpp.r(0, C_GF + kc, C_GF + kc + 1)], [Bh.r(kc, ra, rb)],
                       scale=pp[:, C_GF + kc:C_GF + kc + 1])
                S.add("pool", lambda e, kc=kc: e.tensor_tensor(hT[:, kc, ra:rb], hT[:, kc, ra:rb], rs[:, ra:rb], ALU.mult),
                      reads=[Bh.r(kc, ra, rb), Brs.r(0, ra, rb)], writes=[Bh.r(kc, ra, rb)])
            S.add("sp", lambda e, kc=kc: e.dma_start(out=outv[:, kc, ra - HALO:rb - HALO], in_=hT[:, kc, ra:rb]),
                  reads=[Bh.r(kc, ra, rb)], dma="o")

        def final_post(ra, rb, use_pool):
            rms_stats_blocks(blocks_of(ra, rb))
            for kc in range(KC):
                final_scale(ra, rb, kc, use_pool)

        def ffn2_after(pi):
            if pi == 0:
                tickq.append(("slot", lambda: None))
                tickq.append(("slot", lambda: rms_stats_blocks(blocks_of(HALO, 800))))
                for k2 in range(0, KC, 2):
                    tickq.append(("slot", lambda k2=k2: (final_scale(HALO, 800, k2, False), final_scale(HALO, 800, k2 + 1, False))))
            else:
                flush_ticks()
                final_post(800, T, True)

        ffn(2, w2_in, w2_out, HALO, T, last_split=[(HALO, 800), (800, T)], after_range=ffn2_after,
            after_chunk=lambda pi, d: rms_square(d, *((HALO, 800), (800, T))[pi]))
        n_out = S.dmacnt["o"]

        S.finalize()
        stats = {e: len(S.ops[e]) for e in S.ENGS}

        def run_engine(ename, eng):
            ops = S.ops[ename]
            ms = S.msnum[ename]
            for i, op in enumerate(ops):
                for (key, val) in op["waits"]:
                    if key[0] == "e":
                        eng.wait_ge(sem_e[key[1]], S.msnum[key[1]][val])
                    else:
                        eng.wait_ge(dma_sems[key[1]], val)
                ins = op["emit"](eng)
                if op["dma"] is not None:
                    ins.then_inc(dma_sems[op["dma"]], 16)
                elif op["ms"]:
                    ins.then_inc(sem_e[ename], 1)

        with nc.Block() as block:
            @block.tensor
            def _(e):
                run_engine("pe", e)

            @block.scalar
            def _(e):
                run_engine("act", e)

            @block.vector
            def _(e):
                run_engine("dve", e)

            @block.gpsimd
            def _(e):
                run_engine("pool", e)

            @block.sync
            def _(e):
                run_engine("sp", e)
                e.wait_ge(sem_o, n_out)
    return nc, stats


_CACHE = {}


def _prep_inputs(inp):
    x = np.asarray(inp["x"], dtype=np.float32)
    c = np.asarray(inp["c"], dtype=np.float32)
    b_ada = np.asarray(inp["b_ada"], dtype=np.float32)
    conv_w = np.asarray(inp["conv_w"], dtype=np.float32)

    def fm(v):
        v = np.asarray(v, dtype=np.float32).reshape(-1)
        return np.ascontiguousarray(v.reshape(-1, P).T)

    def tile_cols(w, ncols):
        w = np.asarray(w, dtype=np.float32)
        K, N = w.shape
        return np.ascontiguousarray(w.reshape(K // P, P, N // ncols, ncols).transpose(2, 1, 0, 3))

    wa = np.asarray(inp["w_ada"], dtype=np.float32)
    wa_t = np.ascontiguousarray(wa[:, :4096].reshape(2, 8, P, 8, 512).transpose(3, 0, 2, 1, 4)).reshape(16, P, 8, 512)
    wa_d = np.ascontiguousarray(wa[:, 4096:].T.reshape(56, 2, P, D).transpose(0, 2, 1, 3))
    wb = np.asarray(inp["w_b_group"], dtype=np.float32)
    wb_t = np.ascontiguousarray(wb.reshape(4, 2, P, 2, 256).transpose(0, 3, 2, 1, 4)).reshape(8, P, 2, 256)
    shared = {
        "w_ada": wa_t,
        "w_adaD": wa_d,
        "w1_in": tile_cols(inp["w1_in"], 256),
        "w1_out": tile_cols(inp["w1_out"], 256),
        "w_in": tile_cols(inp["w_in"], 256),
        "w_a_out": tile_cols(inp["w_a_out"], 256),
        "w_b": wb_t,
        "w_out": tile_cols(inp["w_out"], 256),
        "w2_in": tile_cols(inp["w2_in"], 256),
        "w2_out": tile_cols(inp["w2_out"], 256),
    }
    in_maps = []
    for core in range(NCORES):
        b = core // 4
        ch = core % 4
        t0 = ch * TOWN
        xs = np.zeros((T, D), dtype=np.float32)
        if ch > 0:
            xs[:HALO] = x[b, t0 - HALO:t0]
        xs[HALO:] = x[b, t0:t0 + TOWN]
        ppv = np.zeros((P, NPP), dtype=np.float32)
        ppv[:, C_C:C_C + 16] = fm(c[b])
        ppv[:, C_BADA:C_BADA + 144] = fm(b_ada)
        ppv[:, C_G1:C_G1 + 16] = fm(inp["g_ffn1"])
        ppv[:, C_GM:C_GM + 16] = fm(inp["g_mix"])
        ppv[:, C_G2:C_G2 + 16] = fm(inp["g_ffn2"])
        ppv[:, C_GF:C_GF + 16] = fm(inp["g_final"])
        ppv[:, C_CB:C_CB + 8] = fm(inp["conv_b"])
        ppv[:, C_LG:C_LG + 8] = fm(inp["ln_a_g"])
        ppv[:, C_LB:C_LB + 8] = fm(inp["ln_a_b"])
        ppv[:, C_BA:C_BA + 16] = fm(inp["b_a_out"])
        ppv[:, C_BB:C_BB + 16] = fm(np.asarray(inp["b_b_group"]).reshape(-1))
        ppv[:, C_LS:C_LS + 16] = fm(inp["ls_b"])
        cw = conv_w.reshape(31, 8, P).transpose(2, 1, 0).reshape(P, 8 * 31)
        ppv[:, C_CW:C_CW + 248] = cw
        ppv[:, C_HM] = 0.0 if ch == 0 else 1.0
        for g, wdw in enumerate(POOL_W):
            for i in range(16):
                cnt = min(i + 1, wdw) if ch == 0 else wdw
                ppv[:, C_CNT + g * 16 + i] = np.float32(1.0) / np.float32(cnt)
        ppv[:, C_ID:C_ID + P] = np.eye(P, dtype=np.float32)
        m = {"xT": np.ascontiguousarray(xs.T), "pp": ppv,
             "cbc": np.ascontiguousarray(np.broadcast_to(c[b][None, :], (P, D)))}
        m.update(shared)
        in_maps.append(m)
    return in_maps


def kernel(**inputs):
    if "nc" not in _CACHE:
        _CACHE["nc"], _CACHE["stats"] = build_nc()
    nc = _CACHE["nc"]
    in_maps = _prep_inputs(inputs)
    res = run_bass_kernel_spmd(nc, in_maps, core_ids=list(range(NCORES)))
    out = np.empty((2, 4096, D), dtype=np.float32)
    for core in range(NCORES):
        b = core // 4
        ch = core % 4
        out[b, ch * TOWN:(ch + 1) * TOWN, :] = res.results[core]["outT"].T
    return out
```
